# Optimizing a Trainium2 kernel written in Bass

```python
import math
import jax, jax.numpy as jnp
from jax import lax
import numpy as np

D_MODEL = 1024
BATCH = 8
SEQ = 2048
DEPTH = 2

N_MIXERS = 2
CHUNK = 128
N_MEM = 256
D_INNER = 2 * D_MODEL
A_GROUPS = 8
SSM_HEAD_DIM = 64
SSM_HEADS = D_INNER // SSM_HEAD_DIM
SSM_GROUPS = 4
SSM_HPG = SSM_HEADS // SSM_GROUPS
SSM_STATE = 128
CONV_K = 4
CONV_DIM = D_INNER + 2 * SSM_GROUPS * SSM_STATE
X_HEADS = 4
X_HEAD_DIM = 256
X_WIDTH = X_HEADS * X_HEAD_DIM
MIX_OUT = D_INNER + X_WIDTH
D_FF = 4 * D_MODEL
A_IN = 2 * D_INNER + X_WIDTH
B_IN = D_INNER + CONV_DIM + SSM_HEADS + X_WIDTH
EPS = 1e-6

kernel_name = "hybrid_gmlp_ssd_memxattn"


def rms_norm(x, g):
    xf = x.astype(jnp.float32)
    y = xf * lax.rsqrt(jnp.mean(xf * xf, axis=-1, keepdims=True) + EPS)
    return (y * g.astype(jnp.float32)).astype(x.dtype)


def layer_norm(x, g, b):
    xf = x.astype(jnp.float32)
    mu = jnp.mean(xf, axis=-1, keepdims=True)
    xc = xf - mu
    y = xc * lax.rsqrt(jnp.mean(xc * xc, axis=-1, keepdims=True) + EPS)
    return (y * g.astype(jnp.float32) + b.astype(jnp.float32)).astype(x.dtype)


def gmlp_spatial_gating(u, v, ln_g, ln_b, ws, bs):
    bn, s, _ = u.shape
    v = layer_norm(v, ln_g, ln_b)
    v = v.reshape(bn, s // CHUNK, CHUNK, A_GROUPS, D_INNER // A_GROUPS)
    causal = jnp.tril(jnp.ones((CHUNK, CHUNK), dtype=bool))
    w = jnp.where(causal[None], ws, jnp.zeros_like(ws))
    sv = jnp.einsum('gts,bcsgd->bctgd', w, v) + bs.T[:, :, None]
    return u * sv.reshape(bn, s, D_INNER)


def causal_dwconv(x, w, b):
    y = lax.conv_general_dilated(
        x, w[:, None, :], window_strides=(1,), padding=[(CONV_K - 1, 0)],
        dimension_numbers=('NWC', 'WIO', 'NWC'), feature_group_count=x.shape[-1])
    return y + b


def ssd_mixer(zxbcdt, conv_w, conv_b, dt_bias, a_log, d_skip, gnorm):
    bn, s, _ = zxbcdt.shape
    nc = s // CHUNK
    z = zxbcdt[..., :D_INNER]
    xbc = zxbcdt[..., D_INNER:D_INNER + CONV_DIM]
    dt = zxbcdt[..., D_INNER + CONV_DIM:]
    xbc = jax.nn.silu(causal_dwconv(xbc, conv_w, conv_b))
    gn = SSM_GROUPS * SSM_STATE
    xs = xbc[..., :D_INNER].astype(jnp.float32)
    bm = xbc[..., D_INNER:D_INNER + gn].astype(jnp.float32)
    cm = xbc[..., D_INNER + gn:].astype(jnp.float32)

    x = xs.reshape(bn, nc, CHUNK, SSM_GROUPS, SSM_HPG, SSM_HEAD_DIM)
    bm = bm.reshape(bn, nc, CHUNK, SSM_GROUPS, SSM_STATE)
    cm = cm.reshape(bn, nc, CHUNK, SSM_GROUPS, SSM_STATE)
    dt = jax.nn.softplus(dt.astype(jnp.float32) + dt_bias.astype(jnp.float32))
    dt = dt.reshape(bn, nc, CHUNK, SSM_GROUPS, SSM_HPG)
    a = -jnp.exp(a_log.astype(jnp.float32)).reshape(SSM_GROUPS, SSM_HPG)
    da = jnp.transpose(dt * a, (0, 3, 4, 1, 2))
    xdt = x * dt[..., None]

    cs = jnp.cumsum(da, axis=-1)
    causal = jnp.tril(jnp.ones((CHUNK, CHUNK), dtype=bool))
    seg = cs[..., :, None] - cs[..., None, :]
    lmat = jnp.exp(jnp.where(causal, seg, -jnp.inf))

    cb = jnp.einsum('bclgn,bcsgn->bcgls', cm, bm)
    y_diag = jnp.einsum('bcgls,bgrcls,bcsgrp->bclgrp', cb, lmat, xdt)

    decay_states = jnp.exp(cs[..., -1:] - cs)
    states = jnp.einsum('bclgn,bgrcl,bclgrp->bcgrpn', bm, decay_states, xdt)
    chunk_decay = jnp.exp(cs[..., -1])

    def step(h, inp):
        st, dec = inp
        return h * dec[..., None, None] + st, h

    h0 = jnp.zeros((bn, SSM_GROUPS, SSM_HPG, SSM_HEAD_DIM, SSM_STATE), jnp.float32)
    _, prev = lax.scan(step, h0, (jnp.moveaxis(states, 1, 0), jnp.moveaxis(chunk_decay, 3, 0)))
    prev = jnp.moveaxis(prev, 0, 1)

    y_off = jnp.einsum('bclgn,bcgrpn,bgrcl->bclgrp', cm, prev, jnp.exp(cs))
    y = y_diag + y_off + x * d_skip.astype(jnp.float32).reshape(SSM_GROUPS, SSM_HPG)[..., None]
    y = y.reshape(bn, s, D_INNER)

    yg = (y * jax.nn.silu(z.astype(jnp.float32))).reshape(bn, s, SSM_GROUPS, D_INNER // SSM_GROUPS)
    yg = yg * lax.rsqrt(jnp.mean(yg * yg, axis=-1, keepdims=True) + EPS)
    y = yg.reshape(bn, s, D_INNER) * gnorm.astype(jnp.float32)
    return y.astype(zxbcdt.dtype)


def memory_attention(q, mem, mem_g, w_kv):
    bn, s, _ = q.shape
    m = rms_norm(mem, mem_g)
    kv = m @ w_kv
    k = kv[..., :X_WIDTH].reshape(bn, N_MEM, X_HEADS, X_HEAD_DIM)
    v = kv[..., X_WIDTH:].reshape(bn, N_MEM, X_HEADS, X_HEAD_DIM)
    qh = q.reshape(bn, s, X_HEADS, X_HEAD_DIM)
    sc = jnp.einsum('bshd,bmhd->bhsm', qh, k).astype(jnp.float32) * (1.0 / math.sqrt(X_HEAD_DIM))
    p = jax.nn.softmax(sc, axis=-1).astype(v.dtype)
    o = jnp.einsum('bhsm,bmhd->bshd', p, v)
    return o.reshape(bn, s, X_WIDTH)


def setup_inputs(seed: int = 0) -> dict:
    key = jax.random.key(seed)
    ks = jax.random.split(key, 32)
    na = (DEPTH + 1) // 2
    nb = DEPTH // 2
    f32 = jnp.float32

    def nrm(k, shape, scale):
        return jax.random.normal(k, shape, f32) * scale

    def gain(k, shape):
        return 1.0 + 0.02 * jax.random.normal(k, shape, f32)

    dt0 = jnp.exp(jax.random.uniform(ks[20], (nb, SSM_HEADS), f32, math.log(1e-3), math.log(1e-1)))
    return {
        "x": jax.random.normal(ks[0], (BATCH, SEQ, D_MODEL), f32),
        "mem": jax.random.normal(ks[1], (BATCH, N_MEM, D_MODEL), f32),
        "norm_mix": gain(ks[2], (DEPTH, D_MODEL)),
        "norm_ffn": gain(ks[3], (DEPTH, D_MODEL)),
        "mem_norm": gain(ks[4], (DEPTH, D_MODEL)),
        "w_kv": nrm(ks[5], (DEPTH, D_MODEL, 2 * X_WIDTH), D_MODEL ** -0.5),
        "w_out": nrm(ks[6], (DEPTH, MIX_OUT, D_MODEL), MIX_OUT ** -0.5),
        "w_ffn1": nrm(ks[7], (DEPTH, D_MODEL, D_FF), D_MODEL ** -0.5),
        "w_ffn2": nrm(ks[8], (DEPTH, D_FF, D_MODEL), D_FF ** -0.5),
        "a_in": nrm(ks[9], (na, D_MODEL, A_IN), D_MODEL ** -0.5),
        "a_ln_g": gain(ks[10], (na, D_INNER)),
        "a_ln_b": nrm(ks[11], (na, D_INNER), 0.02),
        "a_ws": nrm(ks[12], (na, A_GROUPS, CHUNK, CHUNK), 0.5 * CHUNK ** -0.5),
        "a_bs": gain(ks[13], (na, A_GROUPS, CHUNK)),
        "b_in": nrm(ks[14], (nb, D_MODEL, B_IN), D_MODEL ** -0.5),
        "b_conv_w": nrm(ks[15], (nb, CONV_K, CONV_DIM), CONV_K ** -0.5),
        "b_conv_b": nrm(ks[16], (nb, CONV_DIM), 0.02),
        "b_dt_bias": dt0 + jnp.log(-jnp.expm1(-dt0)),
        "b_a_log": jnp.log(jax.random.uniform(ks[17], (nb, SSM_HEADS), f32, 1.0, 16.0)),
        "b_d": gain(ks[18], (nb, SSM_HEADS)),
        "b_gnorm": gain(ks[19], (nb, D_INNER)),
        "final_norm": gain(ks[21], (D_MODEL,)),
    }


def reference(x, mem, norm_mix, norm_ffn, mem_norm, w_kv, w_out, w_ffn1, w_ffn2,
              a_in, a_ln_g, a_ln_b, a_ws, a_bs,
              b_in, b_conv_w, b_conv_b, b_dt_bias, b_a_log, b_d, b_gnorm,
              final_norm):
    h = x
    for i in range(DEPTH):
        j = i // N_MIXERS
        a = rms_norm(h, norm_mix[i])
        if i % N_MIXERS == 0:
            proj = a @ a_in[j]
            u = jax.nn.gelu(proj[..., :D_INNER], approximate=False)
            v = jax.nn.gelu(proj[..., D_INNER:2 * D_INNER], approximate=False)
            q = proj[..., 2 * D_INNER:]
            mix = gmlp_spatial_gating(u, v, a_ln_g[j], a_ln_b[j], a_ws[j], a_bs[j])
        else:
            proj = a @ b_in[j]
            ssm_cols = D_INNER + CONV_DIM + SSM_HEADS
            mix = ssd_mixer(proj[..., :ssm_cols], b_conv_w[j], b_conv_b[j], b_dt_bias[j],
                            b_a_log[j], b_d[j], b_gnorm[j])
            q = proj[..., ssm_cols:]
        mo = memory_attention(q, mem, mem_norm[i], w_kv[i])
        h = h + jnp.concatenate([mix, mo], axis=-1) @ w_out[i]
        f = rms_norm(h, norm_ffn[i])
        h = h + jnp.square(jax.nn.relu(f @ w_ffn1[i])) @ w_ffn2[i]
    return rms_norm(h, final_norm)
```

```python
import numpy as np
from contextlib import ExitStack
import concourse.bass as bass
import concourse.mybir as mybir
from concourse.bass_utils import run_bass_kernel_spmd

F32 = mybir.dt.float32
BF16 = mybir.dt.bfloat16
FP16 = mybir.dt.float16
AF = mybir.ActivationFunctionType
ALU = mybir.AluOpType
AX = mybir.AxisListType

T = 2048
D = 1024
NT = 16
EPS = 1e-6
NSLOT = 4
STOP = 0
MARKS = []
DEBUG = False

CV_GMIX, CV_GFFN, CV_GMEM, CV_LNG, CV_GNORM, CV_CONVB, CV_CONVW, CV_DCOL, NCV = 0, 16, 32, 48, 64, 80, 104, 200, 216
BC_D, BC_DTB, BC_ALOG, NBC = 0, 32, 64, 96


class KB:
    def __init__(self, nc, es):
        self.nc = nc
        self.es = es
        self.eng = {'pe': nc.tensor, 'act': nc.scalar, 'dve': nc.vector, 'pool': nc.gpsimd, 'sp': nc.sync}
        self.sem = {}
        self.cnt = {}
        for e in ['pe', 'act', 'dve', 'pool']:
            self.sem[e] = es.enter_context(nc.semaphore('s_' + e))
            self.cnt[e] = 0
        self.obs = {e: {} for e in self.eng}
        self.lw = {}
        self.rd = {}
        self.nwait = 0
        self.npe = 0
        self.marks = []

    def dma_sem(self, name):
        if name not in self.sem:
            self.sem[name] = self.es.enter_context(self.nc.semaphore('d_' + name))
            self.cnt[name] = 0
        return name

    def _deps(self, e, reads, writes):
        deps = {}
        for k in reads:
            m = self.lw.get(k)
            if m is not None:
                deps[m[0]] = max(deps.get(m[0], 0), m[1])
        for k in writes:
            m = self.lw.get(k)
            if m is not None:
                deps[m[0]] = max(deps.get(m[0], 0), m[1])
            for s, v in self.rd.get(k, {}).items():
                deps[s] = max(deps.get(s, 0), v)
        for s, v in deps.items():
            if s == e and e == 'pe':
                continue
            if self.obs[e].get(s, 0) >= v:
                continue
            assert v <= self.cnt[s], f"dep on pending signal {s}:{v} > {self.cnt[s]}"
            self.eng[e].wait_ge(self.sem[s], v)
            self.obs[e][s] = v
            self.nwait += 1

    def op(self, e, fn, reads=(), writes=(), signal=True):
        self._deps(e, reads, writes)
        ins = fn()
        mark = (e, self.cnt[e] + 1)
        if signal:
            ins.then_inc(self.sem[e], 1)
            self.cnt[e] += 1
        for k in writes:
            self.lw[k] = mark
            self.rd[k] = {}
        for k in reads:
            d = self.rd.setdefault(k, {})
            d[e] = max(d.get(e, 0), mark[1])
        return ins

    def dma(self, q, out, in_, sem, reads=(), writes=()):
        self.dma_sem(sem)
        self._deps(q, reads, writes)
        self.eng[q].dma_start(out=out, in_=in_).then_inc(self.sem[sem], 16)
        self.cnt[sem] += 16
        mark = (sem, self.cnt[sem])
        for k in writes:
            self.lw[k] = mark
            self.rd[k] = {}
        for k in reads:
            d = self.rd.setdefault(k, {})
            d[sem] = max(d.get(sem, 0), mark[1])

    def barrier(self):
        for e in self.eng:
            if e == 'pool':
                continue
            for s in list(self.sem.keys()):
                if s == e:
                    continue
                v = self.cnt[s]
                if v > 0 and self.obs[e].get(s, 0) < v:
                    self.eng[e].wait_ge(self.sem[s], v)
                    self.obs[e][s] = v

    def mark(self, name):
        self.marks.append((name, self.npe))

    def mm(self, out, lhsT, rhs, start, stop, reads, writes):
        self.npe += 2 if lhsT.dtype == F32 else 1
        return self.op('pe', lambda: self.nc.tensor.matmul(out, lhsT, rhs, start=start, stop=stop),
                       reads=reads, writes=writes, signal=stop)

    def tr(self, out, in_, ident, reads, writes, signal):
        self.npe += 1
        return self.op('pe', lambda: self.nc.tensor.transpose(out, in_, ident),
                       reads=reads, writes=writes, signal=signal)

    def act(self, out, in_, func, reads, writes, bias=None, scale=None, accum_out=None):
        kw = {}
        if bias is not None:
            kw['bias'] = bias
        if scale is not None:
            kw['scale'] = scale
        if accum_out is not None:
            kw['accum_out'] = accum_out
        return self.op('act', lambda: self.nc.scalar.activation(out, in_, func, **kw), reads=reads, writes=writes)

    def tt(self, e, out, in0, in1, op, reads, writes):
        return self.op(e, lambda: self.eng[e].tensor_tensor(out, in0, in1, op), reads=reads, writes=writes)

    def ts(self, e, out, in0, s1, s2, op0, op1, reads, writes):
        if op1 is None:
            s2, op1 = 0.0, ALU.add
        return self.op(e, lambda: self.eng[e].tensor_scalar(out, in0, s1, s2, op0, op1), reads=reads, writes=writes)

    def stt(self, e, out, in0, scalar, in1, op0, op1, reads, writes):
        return self.op(e, lambda: self.eng[e].scalar_tensor_tensor(out, in0, scalar, in1, op0, op1),
                       reads=reads, writes=writes)

    def copy(self, e, out, in_, reads, writes):
        if e == 'act':
            return self.op('act', lambda: self.nc.scalar.copy(out, in_), reads=reads, writes=writes)
        return self.op(e, lambda: self.eng[e].tensor_copy(out, in_), reads=reads, writes=writes)


def bcast(ap, axis, n):
    a = ap.unsqueeze(axis)
    shp = list(a.shape)
    shp[axis] = n
    return a.broadcast_to(shp)


def build_program(dbg=None, skip_l1_mixer=False, n_layers=2, phases=('mem', 'mix', 'ffn')):
    nc = bass.Bass("TRN2", target_bir_lowering=False)
    dr = {}

    def din(name, shape):
        dr[name] = nc.dram_tensor(name, list(shape), F32, kind="ExternalInput").ap()
        return dr[name]

    x = din("x", [T, D])
    mem = din("mem", [256, D])
    w_kv = din("w_kv", [2, 1024, 2048])
    w_out = din("w_out", [2, 3072, 1024])
    w_ffn1 = din("w_ffn1", [2, 1024, 4096])
    w_ffn2 = din("w_ffn2", [2, 4096, 1024])
    a_in = din("a_in", [1024, 5120])
    b_in = din("b_in", [1024, 6176])
    cv_d = din("cv", [128, NCV])
    bc_d = din("bc", [128, NBC])
    wsT_d = din("wsT", [128, 8, 128])
    bsr_d = din("bsr", [1, 1024])
    cst_d = din("cst", [128, 4, 128])
    lnb_d = din("lnb", [128, 2048])
    gfin_d = din("gfin", [128, 1024])
    out = nc.dram_tensor("out", [T, D], F32, kind="ExternalOutput").ap()
    dbg_out = None
    if dbg is not None:
        dbg_out = nc.dram_tensor("dbg", list(dbg[1]), F32, kind="ExternalOutput").ap()

    with ExitStack() as es:
        k = KB(nc, es)

        uid = [0]
        dumped = set()

        def dump(tag, ap, reads):
            if not DEBUG or tag in dumped:
                return
            dumped.add(tag)
            dt_ = nc.dram_tensor("dbg_" + tag, list(ap.shape), ap.dtype, kind="ExternalOutput").ap()
            k.dma('sp', out=dt_, in_=ap, sem="dbg", reads=reads)

        def sb(name, shape, dt, stack=es):
            uid[0] += 1
            return stack.enter_context(nc.sbuf_tensor(f"sb_{name}_{uid[0]}", list(shape), dt))

        h = sb("h", [128, NT, D], F32)
        ring = [sb(f"ring{i}", [128, 8, 512], BF16) for i in range(NSLOT)]
        cv = sb("cv", [128, NCV], F32)
        bc = sb("bc", [128, NBC], F32)
        cstf = sb("cstf", [128, 4, 128], F32)
        cstb = sb("cstb", [128, 4, 128], BF16)
        kT = sb("kT", [128, 8, 256], BF16)
        vmem = sb("vmem", [128, 2, 1024], BF16)
        wdt = sb("wdt", [128, 8, 32], BF16)
        ssq = sb("ssq", [128, 128], F32)
        rstd = sb("rstd", [128, 128], F32)
        ybf = [sb(f"ybf{i}", [128, 1024], BF16) for i in range(2)]
        junk = sb("junk", [128, 1024], BF16)
        ps = [es.enter_context(nc.psum_tensor(f"ps{i}", [128, 512], F32)) for i in range(8)]
        ident_b = cstb[:, 0, :]
        triu_b = cstb[:, 1, :]
        ones_b = cstb[:, 3, :]
        gtl_b = cstb[:, 2, :]
        triu_f = cstf[:, 1, :]
        gtl_f = cstf[:, 2, :]
        ones_f = cstf[:, 3, :]

        state = {'ps': 0, 'slot': 0, 'col': 0, 'yb': 0, 'sv': 0, 'rl': 0}

        def nps(pool=(0, 1, 2, 3, 4, 5, 6, 7)):
            i = pool[state['ps'] % len(pool)]
            state['ps'] += 1
            return i

        def psf(i):
            return ps[i][:, :]

        def psb(i):
            return ps[i][:, :].bitcast(BF16)

        def load_unit(src_ap, view=None):
            s = state['slot'] % NSLOT
            state['slot'] += 1
            dst = ring[s][:, :, :] if view is None else ring_view(s, view)
            k.dma('pool', out=dst, in_=src_ap, sem=f"r{s}", writes=[f"ring{s}"])
            return s

        def ring_view(s, view):
            a, b = view
            return ring[s][:, :, :].rearrange("p c n -> p (c n)").rearrange("p (a b) -> p a b", a=a)

        def wcols(w2d, c0, ncols=512):
            return w2d[:, c0:c0 + ncols].rearrange("(c p) n -> p c n", p=128)

        k.dma('sp', out=cv[:, :], in_=cv_d[:, :], sem="c0", writes=["cv"])
        k.dma('sp', out=bc[:, :], in_=bc_d[:, :], sem="c0", writes=["bc"])
        k.dma('sp', out=cstf[:, :, :], in_=cst_d[:, :, :], sem="c0", writes=["cstf"])
        k.dma('pool', out=cstb[:, :, :], in_=cst_d[:, :, :], sem="c1", writes=["cstb"])
        for key in ["cv", "bc", "cstf"]:
            k.lw[key] = ("c0", k.cnt["c0"])
        k.dma('pool', out=wdt[:, :, :], in_=b_in[:, 5120:5152].rearrange("(c p) n -> p c n", p=128),
              sem="wd", writes=["wdt"])
        x_v = x.rearrange("(t p) d -> p t d", p=128)

        def load_x(gs=(0, 1, 2, 3)):
            for g in gs:
                k.dma('sp', out=h[:, 4 * g:4 * g + 4, :], in_=x_v[:, 4 * g:4 * g + 4, :], sem=f"x{g}",
                      writes=[f"h{t}" for t in range(4 * g, 4 * g + 4)])

        k.op('dve', lambda: nc.vector.memset(ssq[:, :], 0.0), writes=["ssq"])

        presq = {}

        def sq_tiles(tiles):
            n = len(tiles)
            c0 = state['col']
            state['col'] += n
            for j, tt in enumerate(tiles):
                k.act(junk[:, :], h[:, tt, :], AF.Square, reads=[f"h{tt}", "ssq"], writes=["junk", f"ssq{c0 + j}"],
                      accum_out=ssq[:, c0 + j:c0 + j + 1])
            rkeys = [f"rstd{c0 + j}" for j in range(n)]
            k.act(rstd[:, c0:c0 + n], ssq[:, c0:c0 + n], AF.Ln, reads=[f"ssq{c0 + j}" for j in range(n)],
                  writes=rkeys, scale=1.0 / D, bias=EPS)
            k.act(rstd[:, c0:c0 + n], rstd[:, c0:c0 + n], AF.Exp, reads=rkeys, writes=rkeys, scale=-0.5)
            for j, tt in enumerate(tiles):
                presq[tt] = c0 + j

        def norm_T(src_fn, src_key_fn, ntiles, gcol, dst, dst_key_fn, dcols=D, cols=None):
            if cols is None:
                c0 = state['col']
                state['col'] += ntiles
                for j in range(ntiles):
                    k.act(junk[:, :], src_fn(j), AF.Square, reads=[src_key_fn(j), "ssq"],
                          writes=["junk", f"ssq{c0 + j}"], accum_out=ssq[:, c0 + j:c0 + j + 1])
                rkeys = [f"rstd{c0 + j}" for j in range(ntiles)]
                k.act(rstd[:, c0:c0 + ntiles], ssq[:, c0:c0 + ntiles], AF.Ln,
                      reads=[f"ssq{c0 + j}" for j in range(ntiles)], writes=rkeys, scale=1.0 / dcols, bias=EPS)
                k.act(rstd[:, c0:c0 + ntiles], rstd[:, c0:c0 + ntiles], AF.Exp, reads=rkeys, writes=rkeys, scale=-0.5)
                cols = [c0 + j for j in range(ntiles)]

            def stage1(j):
                col = cols[j]
                yb = state['yb'] % 2
                state['yb'] += 1
                k.ts('dve', ybf[yb][:, :], src_fn(j), rstd[:, col:col + 1], None, ALU.mult, None,
                     reads=[src_key_fn(j), f"rstd{col}"], writes=[f"ybf{yb}"])
                pi = nps()
                pv = psb(pi).rearrange("p (c n) -> p c n", c=8)
                for c in range(8):
                    k.tr(pv[:, c, :], ybf[yb][:, c * 128:(c + 1) * 128], ident_b,
                         reads=[f"ybf{yb}", "cstb"], writes=[f"ps{pi}"], signal=(c == 7))
                return pi, pv

            def stage2(j, pi, pv):
                k.tt('dve', dst[:, :, j * 128:(j + 1) * 128], pv,
                     bcast(cv[:, gcol:gcol + 8], 2, 128), ALU.mult,
                     reads=[f"ps{pi}", "cv"], writes=[dst_key_fn(j)])

            prev = stage1(0)
            for j in range(ntiles):
                nxt = stage1(j + 1) if j + 1 < ntiles else None
                stage2(j, *prev)
                prev = nxt

        def hacc(tt, ch, pi):
            k.tt('dve', h[:, tt, ch * 512:(ch + 1) * 512], h[:, tt, ch * 512:(ch + 1) * 512], psf(pi), ALU.add,
                 reads=[f"ps{pi}", f"h{tt}"], writes=[f"h{tt}"])

        def phase_mem(i):
            k.mark(f"mem{i}")
            with ExitStack() as st:
                memt = sb("memt", [128, 2, 1024], F32, st)
                mT = sb("mT", [128, 8, 256], BF16, st)
                k.dma('sp', out=memt[:, :, :], in_=mem.rearrange("(t p) d -> p t d", p=128), sem="mm",
                      writes=["memt0", "memt1"])
                if i == 0:
                    load_x((0, 1))
                norm_T(lambda j: memt[:, j, :], lambda j: f"memt{j}", 2, CV_GMEM + 8 * i, mT,
                       lambda j: f"mT{j}")
                for ku in range(2):
                    s = load_unit(wcols(w_kv[i], ku * 512))
                    for fb in range(4):
                        pi = nps()
                        for c in range(8):
                            k.mm(ps[pi][:, 0:256], ring[s][:, c, fb * 128:(fb + 1) * 128], mT[:, c, :],
                                 c == 0, c == 7, reads=[f"ring{s}", "mT0", "mT1"], writes=[f"ps{pi}"])
                        k.copy('act', kT[:, ku * 4 + fb, :], ps[pi][:, 0:256], reads=[f"ps{pi}"],
                               writes=[f"kT{ku * 4 + fb}"])
                for vu in range(2):
                    s = load_unit(wcols(w_kv[i], 1024 + vu * 512))
                    for mt in range(2):
                        pi = nps()
                        for c in range(8):
                            k.mm(psf(pi), mT[:, c, mt * 128:(mt + 1) * 128], ring[s][:, c, :],
                                 c == 0, c == 7, reads=[f"ring{s}", f"mT{mt}"], writes=[f"ps{pi}"])
                        k.copy('act', vmem[:, mt, vu * 512:(vu + 1) * 512], psf(pi), reads=[f"ps{pi}"],
                               writes=[f"vmem{mt}_{vu}"])
                k.barrier()

        VM_KEYS = [f"vmem{mt}_{vu}" for mt in range(2) for vu in range(2)]

        def attention(i, hf, xT, qcol_w, qcol0, st):
            qT = sb("qT", [128, 2, 2, 1024], BF16, st)
            moT = sb("moT", [128, 8, 1024], BF16, st)
            PT = sb("PT", [128, 2, 2, 512], BF16, st)
            rden = sb("rden", [128, 2, 512], F32, st)
            XK = [f"xT{j}" for j in range(8)]
            pcount = 0
            for qu in range(2):
                s = load_unit(wcols(qcol_w, qcol0 + qu * 512))
                for hl in range(2):
                    hh = 2 * qu + hl
                    qb = hh % 2
                    for fb in range(2):
                        for nh in range(2):
                            pi = nps()
                            for c in range(8):
                                k.mm(psf(pi), ring[s][:, c, hl * 256 + fb * 128: hl * 256 + (fb + 1) * 128],
                                     xT[:, c, nh * 512:(nh + 1) * 512], c == 0, c == 7,
                                     reads=[f"ring{s}"] + XK[nh * 4:nh * 4 + 4], writes=[f"ps{pi}"])
                            k.copy('act', qT[:, qb, fb, nh * 512:(nh + 1) * 512], psf(pi), reads=[f"ps{pi}"],
                                   writes=[f"qT{qb}_{fb}_{nh}"])
                    for nh in range(2):
                        pb = pcount % 2
                        pcount += 1
                        for mb in range(2):
                            pi = nps()
                            for fb in range(2):
                                k.mm(psf(pi), kT[:, hh * 2 + fb, mb * 128:(mb + 1) * 128],
                                     qT[:, qb, fb, nh * 512:(nh + 1) * 512], fb == 0, fb == 1,
                                     reads=[f"kT{hh * 2 + fb}", f"qT{qb}_{fb}_{nh}"], writes=[f"ps{pi}"])
                            k.act(PT[:, pb, mb, :], psf(pi), AF.Exp, reads=[f"ps{pi}"], writes=[f"PT{pb}_{mb}"],
                                  scale=1.0 / 16.0)
                        pi = nps()
                        for mb in range(2):
                            k.mm(psf(pi), ones_b, PT[:, pb, mb, :], mb == 0, mb == 1,
                                 reads=["cstb", f"PT{pb}_{mb}"], writes=[f"ps{pi}"])
                        k.op('dve', lambda: nc.vector.reciprocal(rden[:, pb, :], psf(pi)), reads=[f"ps{pi}"],
                             writes=[f"rden{pb}"])
                        for fb in range(2):
                            pi = nps()
                            for mb in range(2):
                                k.mm(psf(pi), vmem[:, mb, hh * 256 + fb * 128: hh * 256 + (fb + 1) * 128],
                                     PT[:, pb, mb, :], mb == 0, mb == 1,
                                     reads=VM_KEYS + [f"PT{pb}_{mb}"], writes=[f"ps{pi}"])
                            k.tt('dve', moT[:, hh * 2 + fb, nh * 512:(nh + 1) * 512], psf(pi), rden[:, pb, :],
                                 ALU.mult, reads=[f"ps{pi}", f"rden{pb}"], writes=[f"moT{hh * 2 + fb}_{nh}"])
            su = [load_unit(w_out[i][2048:3072, ch * 512:(ch + 1) * 512].rearrange("(c p) n -> p c n", p=128))
                  for ch in range(2)]
            for tt in range(8):
                for ch in range(2):
                    pi = nps()
                    for c in range(8):
                        k.mm(psf(pi), moT[:, c, tt * 128:(tt + 1) * 128], ring[su[ch]][:, c, :], c == 0, c == 7,
                             reads=[f"ring{su[ch]}", f"moT{c}_{tt // 4}"], writes=[f"ps{pi}"])
                    hacc(hf * 8 + tt, ch, pi)

        def layer0_mixer():
            with ExitStack() as L:
                WT = sb("WT", [128, 8, 128], BF16, L)
                Cst = sb("Cst", [128, 16, 128], F32, L)
                with ExitStack() as st:
                    wsf = sb("wsf", [128, 8, 128], F32, st)
                    lnb = sb("lnb", [128, 2048], BF16, st)
                    bsr = sb("bsr", [1, 1024], BF16, st)
                    bsrf = sb("bsrf", [1, 1024], F32, st)
                    k.dma('sp', out=wsf[:, :, :], in_=wsT_d[:, :, :], sem="m0", writes=["wsf"])
                    k.dma('sp', out=bsrf[:, :], in_=bsr_d[:, :], sem="m1", writes=["bsrf"])
                    k.op('dve', lambda: nc.vector.tensor_copy(bsr[:, :], bsrf[:, :]), reads=["bsrf"], writes=["bsr"])
                    lnbf = sb("lnbf", [128, 2048], F32, st)
                    k.dma('sp', out=lnbf[:, :], in_=lnb_d[:, :], sem="m2", writes=["lnbf"])
                    load_x((2, 3))
                    k.op('dve', lambda: nc.vector.tensor_copy(lnb[:, :], lnbf[:, :]),
                         reads=["lnbf"], writes=["lnb"])
                    k.tt('dve', WT[:, :, :], wsf[:, :, :], bcast(triu_f, 1, 8), ALU.mult,
                         reads=["wsf", "cstf"], writes=["WT"])
                    for db in range(16):
                        g = db // 2
                        pi = nps()
                        k.mm(ps[pi][:, 0:128], lnb[:, db * 128:(db + 1) * 128], WT[:, g, :], True, False,
                             reads=["lnb", "WT"], writes=[f"ps{pi}"])
                        k.mm(ps[pi][:, 0:128], ones_b[0:1, :], bsr[0:1, g * 128:(g + 1) * 128], False, True,
                             reads=["cstb", "bsr"], writes=[f"ps{pi}"])
                        k.copy('act', Cst[:, db, :], ps[pi][:, 0:128], reads=[f"ps{pi}"], writes=["Cst"])
                    k.barrier()
                if STOP == 1:
                    return
                for hf in range(2):
                    with ExitStack() as H:
                        k.mark(f"L0h{hf}_norm")
                        xT = sb("xT", [128, 8, 1024], BF16, H)
                        norm_T(lambda j: h[:, hf * 8 + j, :], lambda j: f"h{hf * 8 + j}", 8, CV_GMIX, xT,
                               lambda j: f"xT{j}", cols=[presq.pop(hf * 8 + j) for j in range(8)]
                               if all((hf * 8 + j) in presq for j in range(8)) else None)
                        if hf == 0:
                            sq_tiles(list(range(8, 16)))
                        else:
                            sq_tiles(list(range(0, 8)))
                        XK = [f"xT{j}" for j in range(8)]
                        if STOP == 2:
                            return
                        with ExitStack() as st:
                            k.mark(f"L0h{hf}_attn")
                            attention(0, hf, xT, a_in, 4096, st)
                            k.barrier()
                        if STOP == 3:
                            return
                        with ExitStack() as st:
                            V = sb("V", [128, 8, 2048], BF16, st)
                            U = sb("U", [128, 2, 4, 1024], BF16, st)
                            svt = sb("svt", [128, 2, 512], F32, st)
                            stats = sb("stats", [128, 8, 4, 6], F32, st)
                            mv = sb("mv", [128, 8, 2], F32, st)
                            lr = sb("lr", [128, 8, 2], F32, st)
                            k.mark(f"L0h{hf}_vproj")
                            for vu in range(4):
                                s = load_unit(wcols(a_in, 2048 + vu * 512))
                                for tt in range(8):
                                    pi = nps()
                                    for c in range(8):
                                        k.mm(psf(pi), xT[:, c, tt * 128:(tt + 1) * 128], ring[s][:, c, :],
                                             c == 0, c == 7, reads=[f"ring{s}", f"xT{tt}"], writes=[f"ps{pi}"])
                                    k.act(V[:, tt, vu * 512:(vu + 1) * 512], psf(pi), AF.Gelu, reads=[f"ps{pi}"],
                                          writes=[f"V{tt}"])
                            if STOP == 4:
                                return
                            for tt in range(8):
                                for q4 in range(4):
                                    k.op('dve', lambda: nc.vector.bn_stats(stats[:, tt, q4, :],
                                                                           V[:, tt, q4 * 512:(q4 + 1) * 512]),
                                         reads=[f"V{tt}"], writes=[f"stats{tt}"])
                                k.op('dve', lambda: nc.vector.bn_aggr(mv[:, tt, :], stats[:, tt, :, :]),
                                     reads=[f"stats{tt}"], writes=[f"mv{tt}"])
                            k.act(lr[:, :, 0], mv[:, :, 1], AF.Ln, reads=[f"mv{tt}" for tt in range(8)],
                                  writes=[f"lr{tt}" for tt in range(8)], bias=EPS)
                            k.act(lr[:, :, 0], lr[:, :, 0], AF.Exp, reads=[f"lr{tt}" for tt in range(8)],
                                  writes=[f"lr{tt}" for tt in range(8)], scale=-0.5)
                            for tt in range(8):
                                k.stt('dve', lr[:, tt, 1:2], mv[:, tt, 0:1], -1.0, lr[:, tt, 0:1], ALU.mult,
                                      ALU.mult, reads=[f"mv{tt}", f"lr{tt}"], writes=[f"lr{tt}"])
                                k.ts('dve', V[:, tt, :], V[:, tt, :], lr[:, tt, 0:1], lr[:, tt, 1:2], ALU.mult,
                                     ALU.add, reads=[f"V{tt}", f"lr{tt}"], writes=[f"V{tt}"])
                            if STOP == 5:
                                return
                            gl = {}

                            def g_loads(gp):
                                a = load_unit(wcols(a_in, gp * 512))
                                b_ = load_unit(w_out[0][gp * 512:(gp + 1) * 512, :].rearrange(
                                    "(c p) n -> p c n", p=128), view=(4, 1024))
                                gl[gp] = (a, b_)

                            def g_uproj(gp):
                                k.mark(f"L0h{hf}_gp{gp}")
                                ub = gp % 2
                                s_ = gl[gp][0]
                                for fb in range(4):
                                    for nh in range(2):
                                        pi = nps()
                                        for c in range(8):
                                            k.mm(psf(pi), ring[s_][:, c, fb * 128:(fb + 1) * 128],
                                                 xT[:, c, nh * 512:(nh + 1) * 512], c == 0, c == 7,
                                                 reads=[f"ring{s_}"] + XK[nh * 4:nh * 4 + 4], writes=[f"ps{pi}"])
                                        k.act(U[:, ub, fb, nh * 512:(nh + 1) * 512], psf(pi), AF.Gelu,
                                              reads=[f"ps{pi}"], writes=[f"U{ub}_{fb}_{nh}"])

                            def g_gate(gp):
                                ub = gp % 2
                                for fb in range(4):
                                    db = gp * 4 + fb
                                    g = db // 2
                                    for cq in range(2):
                                        pi = nps()
                                        for c4 in range(4):
                                            ch = cq * 4 + c4
                                            k.mm(ps[pi][:, c4 * 128:(c4 + 1) * 128],
                                                 V[:, ch, db * 128:(db + 1) * 128], WT[:, g, :], True, True,
                                                 reads=[f"V{ch}", "WT"], writes=[f"ps{pi}"])
                                        sb_i = state['sv'] % 2
                                        state['sv'] += 1
                                        k.act(svt[:, sb_i, :], psf(pi), AF.Identity, reads=[f"ps{pi}", "cv"],
                                              writes=[f"svt{sb_i}"], scale=cv[:, CV_LNG + db:CV_LNG + db + 1])
                                        k.tt('dve', svt[:, sb_i, :].rearrange("p (a b) -> p a b", a=4),
                                             svt[:, sb_i, :].rearrange("p (a b) -> p a b", a=4),
                                             bcast(Cst[:, db, :], 1, 4), ALU.add,
                                             reads=[f"svt{sb_i}", "Cst"], writes=[f"svt{sb_i}"])
                                        k.tt('dve', U[:, ub, fb, cq * 512:(cq + 1) * 512],
                                             U[:, ub, fb, cq * 512:(cq + 1) * 512], svt[:, sb_i, :], ALU.mult,
                                             reads=[f"svt{sb_i}", f"U{ub}_{fb}_{cq}"], writes=[f"U{ub}_{fb}_{cq}"])

                            def g_out(gp):
                                ub = gp % 2
                                so_ = gl[gp][1]
                                rv = ring_view(so_, (4, 1024))
                                for tt in range(8):
                                    for ch in range(2):
                                        pi = nps()
                                        for fb in range(4):
                                            k.mm(psf(pi), U[:, ub, fb, tt * 128:(tt + 1) * 128],
                                                 rv[:, fb, ch * 512:(ch + 1) * 512], fb == 0, fb == 3,
                                                 reads=[f"ring{so_}", f"U{ub}_{fb}_{tt // 4}"], writes=[f"ps{pi}"])
                                        hacc(hf * 8 + tt, ch, pi)

                            g_loads(0)
                            g_uproj(0)
                            g_gate(0)
                            for gp in range(1, 4):
                                g_loads(gp)
                                g_uproj(gp)
                                g_out(gp - 1)
                                g_gate(gp)
                            g_out(3)
                            if hf == 1:
                                sq_tiles(list(range(8, 16)))
                            k.barrier()


        def layer1_mixer():
            with ExitStack() as L:
                HAL = sb("HAL", [128, 24, 3], BF16, L)
                prevS = sb("prevS", [128, 4, 512], F32, L)
                A_b = sb("A_b", [128, 32], F32, L)
                gss = sb("gss", [128, 64], F32, L)
                grs = sb("grs", [128, 64], F32, L)
                k.op('dve', lambda: nc.vector.memset(HAL[:, :, :], 0.0), writes=["HAL"])
                k.op('dve', lambda: nc.vector.memset(prevS[:, :, :], 0.0), writes=[f"prevS{g}" for g in range(4)])
                k.op('dve', lambda: nc.vector.memset(gss[:, :], 0.0), writes=["gss"])
                k.act(A_b[:, :], bc[:, BC_ALOG:BC_ALOG + 32], AF.Exp, reads=["bc"], writes=["A_b"])
                k.ts('dve', A_b[:, :], A_b[:, :], -1.0, None, ALU.mult, None, reads=["A_b"], writes=["A_b"])
                gcol = [0]
                for hf in range(2):
                    with ExitStack() as H:
                        k.mark(f"L1h{hf}_norm")
                        xT = sb("xT", [128, 8, 1024], BF16, H)
                        norm_T(lambda j: h[:, hf * 8 + j, :], lambda j: f"h{hf * 8 + j}", 8, CV_GMIX + 8, xT,
                               lambda j: f"xT{j}", cols=[presq.pop(hf * 8 + j) for j in range(8)]
                               if all((hf * 8 + j) in presq for j in range(8)) else None)
                        if hf == 1:
                            sq_tiles(list(range(0, 8)))
                        XK = [f"xT{j}" for j in range(8)]
                        with ExitStack() as st:
                            k.mark(f"L1h{hf}_attn")
                            attention(1, hf, xT, b_in, 5152, st)
                            k.barrier()
                        with ExitStack() as st:
                            Zs = sb("Zs", [128, 8, 512], BF16, st)
                            XB = sb("XB", [128, 6, 1027], BF16, st)
                            XC = sb("XC", [128, 6, 1024], BF16, st)
                            DG = sb("DG", [128, 2, 4, 128], BF16, st)
                            dtsA = sb("dts", [128, 2, 8, 8], F32, st)
                            daA = sb("da", [128, 2, 8, 8], F32, st)
                            Rh = sb("Rh", [128, 2, 4, 128], FP16, st)
                            c16 = sb("c16", [128, 2, 128], FP16, st)
                            dahA = sb("dah", [128, 2, 8, 8], FP16, st)
                            E3 = sb("E3", [128, 3, 8, 8], F32, st)
                            cbm = sb("cbm", [128, 8, 128], BF16, st)
                            E = sb("E", [128, 3, 8, 128], BF16, st)
                            MT = E
                            xdt = sb("xdt", [128, 2, 512], BF16, st)
                            DGD = sb("DGD", [128, 4, 128], BF16, st)
                            xdd = sb("xdd", [128, 2, 512], BF16, st)
                            Bc = sb("Bc", [128, 2, 128], BF16, st)
                            yb = sb("yb", [128, 1, 512], F32, st)
                            ynb = sb("ynb", [128, 2, 512], BF16, st)
                            prevB = sb("prevB", [128, 2, 512], BF16, st)
                            YT = sb("YT", [128, 4, 1024], BF16, st)
                            k.copy('dve', c16[:, :, :], cstf[:, 1:3, :], reads=["cstf"], writes=["c16"])
                            for g in range(4):
                                k.mark(f"L1h{hf}_g{g}_proj")
                                def issue_proj_loads(gg):
                                    a = load_unit(wcols(b_in, gg * 512))
                                    b_ = load_unit(wcols(b_in, 2048 + gg * 512))
                                    c_ = state['slot'] % NSLOT
                                    state['slot'] += 1
                                    k.dma('pool', out=ring[c_][:, :, 0:128],
                                          in_=wcols(b_in, 4096 + gg * 128, 128), sem=f"r{c_}", writes=[f"ring{c_}"])
                                    k.dma('pool', out=ring[c_][:, :, 128:256],
                                          in_=wcols(b_in, 4608 + gg * 128, 128), sem=f"r{c_}", writes=[f"ring{c_}"])
                                    return a, b_, c_

                                def issue_out_load(gg):
                                    return load_unit(w_out[1][gg * 512:(gg + 1) * 512, :].rearrange(
                                        "(c p) n -> p c n", p=128), view=(4, 1024))

                                if g == 0:
                                    pl = issue_proj_loads(0)
                                    pso = issue_out_load(0)
                                sz, sx, s3 = pl
                                so = pso
                                gpar = g % 2
                                dts = dtsA[:, gpar, :, :]
                                da = daA[:, gpar, :, :]
                                dah = dahA[:, gpar, :, :]
                                KD, KA, KH = f"dts{gpar}", f"da{gpar}", f"dah{gpar}"

                                def dt_path(gg, pool=None):
                                    gp_ = gg % 2
                                    pi = nps(pool) if pool else nps()
                                    for tt in range(8):
                                        for c in range(8):
                                            k.mm(ps[pi][:, tt * 8:(tt + 1) * 8], xT[:, c, tt * 128:(tt + 1) * 128],
                                                 wdt[:, c, gg * 8:(gg + 1) * 8], c == 0, c == 7,
                                                 reads=["wdt", f"xT{tt}"], writes=[f"ps{pi}"])
                                    k.tt('dve', dtsA[:, gp_, :, :], ps[pi][:, 0:64].rearrange("p (a b) -> p a b", a=8),
                                         bcast(bc[:, BC_DTB + gg * 8:BC_DTB + gg * 8 + 8], 1, 8), ALU.add,
                                         reads=[f"ps{pi}", "bc"], writes=[f"dts{gp_}"])
                                    k.act(dtsA[:, gp_, :, :], dtsA[:, gp_, :, :], AF.Exp, reads=[f"dts{gp_}"],
                                          writes=[f"dts{gp_}"])
                                    k.act(dtsA[:, gp_, :, :], dtsA[:, gp_, :, :], AF.Ln, reads=[f"dts{gp_}"],
                                          writes=[f"dts{gp_}"], bias=1.0)
                                    k.tt('dve', daA[:, gp_, :, :], dtsA[:, gp_, :, :],
                                         bcast(A_b[:, gg * 8:(gg + 1) * 8], 1, 8), ALU.mult,
                                         reads=[f"dts{gp_}", "A_b"], writes=[f"da{gp_}"])
                                    k.copy('pool', dahA[:, gp_, :, :], daA[:, gp_, :, :], reads=[f"da{gp_}"],
                                           writes=[f"dah{gp_}"])

                                def z_tile(gg, tt, slot, pool=None, silu=True):
                                    pi = nps(pool) if pool else nps()
                                    for c in range(8):
                                        k.mm(psf(pi), xT[:, c, tt * 128:(tt + 1) * 128], ring[slot][:, c, :],
                                             c == 0, c == 7, reads=[f"ring{slot}", f"xT{tt}"], writes=[f"ps{pi}"])
                                    k.act(Zs[:, tt, :], psf(pi), AF.Silu if silu else AF.Identity, reads=[f"ps{pi}"],
                                          writes=[f"Zs{tt}"])

                                if g == 0:
                                    dt_path(0)
                                    for tt in range(8):
                                        z_tile(0, tt, sz)
                                else:
                                    for tt in range(8):
                                        k.act(Zs[:, tt, :], Zs[:, tt, :], AF.Silu, reads=[f"Zs{tt}"], writes=[f"Zs{tt}"])
                                blks = [g * 4 + fb for fb in range(4)] + [16 + g, 20 + g]

                                def xb_proj(gg, slots, pool=None):
                                    sx_, s3_ = slots
                                    bl = [gg * 4 + fb for fb in range(4)] + [16 + gg, 20 + gg]
                                    for fb in range(6):
                                        if fb < 4:
                                            wsl = lambda c: ring[sx_][:, c, fb * 128:(fb + 1) * 128]
                                            wk = f"ring{sx_}"
                                        else:
                                            wsl = lambda c: ring[s3_][:, c, (fb - 4) * 128:(fb - 3) * 128]
                                            wk = f"ring{s3_}"
                                        k.copy('dve', XB[:, fb, 0:3], HAL[:, bl[fb], :], reads=["HAL"],
                                               writes=[f"XB{fb}_0"])
                                        for nh in range(2):
                                            pi = nps(pool) if pool else nps()
                                            for c in range(8):
                                                k.mm(psf(pi), wsl(c), xT[:, c, nh * 512:(nh + 1) * 512], c == 0, c == 7,
                                                     reads=[wk] + XK[nh * 4:nh * 4 + 4], writes=[f"ps{pi}"])
                                            k.copy('dve', XB[:, fb, 3 + nh * 512:3 + (nh + 1) * 512], psf(pi),
                                                   reads=[f"ps{pi}"], writes=[f"XB{fb}_{nh}"])
                                            if nh == 1:
                                                k.copy('dve', HAL[:, bl[fb], :], XB[:, fb, 1024:1027],
                                                       reads=[f"XB{fb}_1"], writes=["HAL"])
                                            yield

                                if g == 0:
                                    for _ in xb_proj(0, (sx, s3)):
                                        pass
                                for fb in range(6):
                                    blk = blks[fb]
                                    dgi = fb % 2
                                    for kk in range(4):
                                        k.ts('dve', DG[:, dgi, kk, :], cstf[:, 0, :],
                                             cv[:, CV_CONVW + blk * 4 + kk:CV_CONVW + blk * 4 + kk + 1], None,
                                             ALU.mult, None, reads=["cstf", "cv"], writes=[f"DG{dgi}"])
                                    for nh in range(2):
                                        pi = nps()
                                        for kk in range(4):
                                            k.mm(psf(pi), DG[:, dgi, kk, :],
                                                 XB[:, fb, kk + nh * 512:kk + nh * 512 + 512], kk == 0, kk == 3,
                                                 reads=[f"DG{dgi}", f"XB{fb}_0", f"XB{fb}_1"] if nh else
                                                 [f"DG{dgi}", f"XB{fb}_0"] + [f"XB{fb}_0"],
                                                 writes=[f"ps{pi}"])
                                        k.act(XC[:, fb, nh * 512:(nh + 1) * 512], psf(pi), AF.Silu,
                                              reads=[f"ps{pi}", "cv"], writes=[f"XC{fb}_{nh}"],
                                              bias=cv[:, CV_CONVB + blk:CV_CONVB + blk + 1])
                                dump("Zs", Zs[:, :, :], [f"Zs{t_}" for t_ in range(8)])
                                dump("XB", XB[:, :, :], [f"XB{f_}_{n_}" for f_ in range(6) for n_ in range(2)])
                                dump("XC", XC[:, :, :], [f"XC{f_}_{n_}" for f_ in range(6) for n_ in range(2)])
                                for fb in range(4):
                                    k.ts('dve', DGD[:, fb, :], cstf[:, 0, :],
                                         cv[:, CV_DCOL + g * 4 + fb:CV_DCOL + g * 4 + fb + 1], None, ALU.mult, None,
                                         reads=["cstf", "cv"], writes=["DGD"])
                                if hf == 1:
                                    k.copy('act', prevB[:, 0, :], prevS[:, g, :], reads=[f"prevS{g}"],
                                           writes=["prevB0"])
                                RP = (4, 5, 6, 7)
                                pe_ = nps()
                                da_all = da.rearrange("p a b -> p (a b)")
                                k.mm(ps[pe_][:, 0:64], triu_f, da_all, True, True, reads=["cstf", KA],
                                     writes=[f"ps{pe_}"])
                                k.mm(ps[pe_][:, 64:128], gtl_f, da_all, True, True, reads=["cstf", KA],
                                     writes=[f"ps{pe_}"])
                                k.mm(ps[pe_][:, 128:192], ones_f, da_all, True, True, reads=["cstf", KA],
                                     writes=[f"ps{pe_}"])
                                k.act(E3[:, :, :, :].rearrange("p t a b -> p (t a b)"), ps[pe_][:, 0:192], AF.Exp,
                                      reads=[f"ps{pe_}"], writes=["E3"])
                                for c4 in range(2):
                                    pc = nps()
                                    for cc in range(4):
                                        c = c4 * 4 + cc
                                        cols = slice(c * 128, (c + 1) * 128)
                                        k.mm(ps[pc][:, cc * 128:(cc + 1) * 128], XC[:, 4, cols], XC[:, 5, cols], True, True,
                                             reads=[f"XC4_{c // 4}", f"XC5_{c // 4}"], writes=[f"ps{pc}"])
                                    k.tt('dve', cbm[:, c4 * 4:(c4 + 1) * 4, :],
                                         psf(pc).rearrange("p (a b) -> p a b", a=4), bcast(triu_f, 1, 4), ALU.mult,
                                         reads=[f"ps{pc}", "cstf"], writes=[f"cbm{c4}"])

                                def st_R(c):
                                    for hh2 in range(2):
                                        k.tt('pool', Rh[:, hh2, :, :], bcast(dah[:, c, hh2 * 4:(hh2 + 1) * 4], 2, 128),
                                             bcast(c16[:, 0, :], 1, 4), ALU.mult, reads=[KH, "c16"], writes=[f"Rh{hh2}"])

                                def st_seg(c):
                                    b2 = c % 3
                                    for hh2 in range(2):
                                        pq = nps(RP)
                                        k.mm(psf(pq), c16[:, 1, :], Rh[:, hh2, :, :].rearrange("p a b -> p (a b)"),
                                             True, True, reads=["c16", f"Rh{hh2}"], writes=[f"ps{pq}"])
                                        k.act(E[:, b2, hh2 * 4:(hh2 + 1) * 4, :].rearrange("p a b -> p (a b)"), psf(pq),
                                              AF.Exp, reads=[f"ps{pq}"], writes=[f"E{b2}_{hh2}", f"MT{b2}"])

                                def st_B1(c):
                                    b2 = c % 2
                                    nh = c // 4
                                    cols = slice(c * 128, (c + 1) * 128)
                                    px = nps(RP)
                                    pxv = psb(px)
                                    for fb in range(4):
                                        k.tr(pxv[:, fb * 128:(fb + 1) * 128], XC[:, fb, cols], ident_b,
                                             reads=[f"XC{fb}_{nh}", "cstb"], writes=[f"ps{px}"], signal=(fb == 3))
                                    pbk = nps(RP)
                                    k.tr(psb(pbk)[:, 0:128], XC[:, 4, cols], ident_b, reads=[f"XC4_{nh}", "cstb"],
                                         writes=[f"ps{pbk}"], signal=True)
                                    e3 = c % 3
                                    k.tt('dve', MT[:, e3, :, :], E[:, e3, :, :], bcast(cbm[:, c, :], 1, 8), ALU.mult,
                                         reads=[f"E{e3}_0", f"E{e3}_1", f"cbm{c // 4}"],
                                         writes=[f"MT{e3}", f"E{e3}_0", f"E{e3}_1"])
                                    x3 = pxv[:, 0:512].rearrange("p (a b) -> p a b", a=8)
                                    k.tt('dve', xdt[:, b2, :].rearrange("p (a b) -> p a b", a=8), x3,
                                         bcast(dts[:, c, :], 2, 64), ALU.mult, reads=[f"ps{px}", KD],
                                         writes=[f"xdt{b2}"])
                                    k.tt('dve', xdd[:, b2, :].rearrange("p (a b) -> p a b", a=8),
                                         xdt[:, b2, :].rearrange("p (a b) -> p a b", a=8),
                                         bcast(E3[:, 1, c, :], 2, 64), ALU.mult, reads=[f"xdt{b2}", "E3"],
                                         writes=[f"xdd{b2}"])
                                    k.copy('act', Bc[:, b2, :], psb(pbk)[:, 0:128], reads=[f"ps{pbk}"],
                                           writes=[f"Bc{b2}"])

                                def st_B2(c):
                                    b2 = c % 2
                                    nh = c // 4
                                    cols = slice(c * 128, (c + 1) * 128)
                                    py = b2
                                    for fb in range(4):
                                        k.mm(ps[py][:, fb * 128:(fb + 1) * 128], XC[:, fb, cols], DGD[:, fb, :], True, False,
                                             reads=[f"XC{fb}_{nh}", "DGD"], writes=[f"ps{py}"])
                                        for hh in (2 * fb, 2 * fb + 1):
                                            k.mm(ps[py][:, hh * 64:(hh + 1) * 64], MT[:, c % 3, hh, :],
                                                 xdt[:, b2, hh * 64:(hh + 1) * 64], False, hh == 7,
                                                 reads=[f"MT{c % 3}", f"xdt{b2}"], writes=[f"ps{py}"])
                                    pst = 2 + b2
                                    k.mm(psf(pst), Bc[:, b2, :], xdd[:, b2, :], True, True,
                                         reads=[f"Bc{b2}", f"xdd{b2}"], writes=[f"ps{pst}"])

                                def st_C_pe(c):
                                    b2 = c % 2
                                    nh = c // 4
                                    cols = slice(c * 128, (c + 1) * 128)
                                    first = (hf == 0 and c == 0)
                                    if first:
                                        return None
                                    po = nps(RP)
                                    k.mm(psf(po), XC[:, 5, cols], prevB[:, b2, :], True, True,
                                         reads=[f"XC5_{nh}", f"prevB{b2}"], writes=[f"ps{po}"])
                                    return po

                                def st_C(c, po):
                                    b2 = c % 2
                                    py = b2
                                    pst = 2 + b2
                                    k.tt('dve', prevS[:, g, :].rearrange("p (a b) -> p a b", a=8),
                                         prevS[:, g, :].rearrange("p (a b) -> p a b", a=8),
                                         bcast(E3[:, 2, c, :], 2, 64), ALU.mult, reads=[f"prevS{g}", "E3"],
                                         writes=[f"prevS{g}"])
                                    k.tt('dve', prevS[:, g, :], prevS[:, g, :], psf(pst), ALU.add,
                                         reads=[f"prevS{g}", f"ps{pst}"], writes=[f"prevS{g}"])
                                    k.copy('act', prevB[:, 1 - b2, :], prevS[:, g, :], reads=[f"prevS{g}"],
                                           writes=[f"prevB{1 - b2}"])
                                    if po is not None:
                                        k.tt('dve', yb[:, 0, :].rearrange("p (a b) -> p a b", a=8),
                                             psf(po).rearrange("p (a b) -> p a b", a=8),
                                             bcast(E3[:, 0, c, :], 2, 64), ALU.mult, reads=[f"ps{po}", "E3"],
                                             writes=["yb0"])
                                        k.tt('dve', yb[:, 0, :], yb[:, 0, :], psf(py), ALU.add,
                                             reads=["yb0", f"ps{py}"], writes=["yb0"])
                                    else:
                                        k.copy('dve', yb[:, 0, :], psf(py), reads=[f"ps{py}"], writes=["yb0"])
                                    gc = gcol[0]
                                    gcol[0] += 1
                                    k.tt('dve', yb[:, 0, :], yb[:, 0, :], Zs[:, c, :], ALU.mult,
                                         reads=["yb0", f"Zs{c}"], writes=["yb0"])
                                    k.act(junk[:, 0:512], yb[:, 0, :], AF.Square, reads=["yb0", "gss"],
                                          writes=["junk", f"gss{gc}"], accum_out=gss[:, gc:gc + 1])
                                    k.act(grs[:, gc:gc + 1], gss[:, gc:gc + 1], AF.Ln, reads=[f"gss{gc}"],
                                          writes=[f"grs{gc}"], scale=1.0 / 512, bias=EPS)
                                    k.act(grs[:, gc:gc + 1], grs[:, gc:gc + 1], AF.Exp, reads=[f"grs{gc}"],
                                          writes=[f"grs{gc}"], scale=-0.5)
                                    k.act(ynb[:, b2, :], yb[:, 0, :], AF.Identity, reads=["yb0", f"grs{gc}"],
                                          writes=[f"ynb{b2}"], scale=grs[:, gc:gc + 1])

                                def st_D(c):
                                    b2 = c % 2
                                    pt = nps(RP)
                                    ptv = psb(pt)
                                    for fb in range(4):
                                        k.tr(ptv[:, fb * 128:(fb + 1) * 128], ynb[:, b2, fb * 128:(fb + 1) * 128],
                                             ident_b, reads=[f"ynb{b2}", "cstb"], writes=[f"ps{pt}"],
                                             signal=(fb == 3))
                                    for fb in range(4):
                                        k.act(YT[:, fb, c * 128:(c + 1) * 128], ptv[:, fb * 128:(fb + 1) * 128],
                                              AF.Identity, reads=[f"ps{pt}", "cv"], writes=[f"YT{c // 4}"],
                                              scale=cv[:, CV_GNORM + g * 4 + fb:CV_GNORM + g * 4 + fb + 1])

                                xgen = None
                                if g < 3:
                                    pl = issue_proj_loads(g + 1)
                                    xgen = xb_proj(g + 1, (pl[1], pl[2]), pool=(4, 5, 6, 7))
                                k.mark(f"L1h{hf}_g{g}_chunks")
                                if g < 3:
                                    dt_path(g + 1, pool=(4, 5, 6, 7))
                                for c0 in range(3):
                                    st_R(c0)
                                    st_seg(c0)
                                st_B1(0)
                                st_B2(0)
                                for i in range(8):
                                    if i + 3 < 8:
                                        st_R(i + 3)
                                    if i + 1 < 8:
                                        st_B1(i + 1)
                                    po = st_C_pe(i)
                                    if i + 1 < 8:
                                        st_B2(i + 1)
                                    st_C(i, po)
                                    if g < 3:
                                        z_tile(g + 1, i, pl[0], pool=(4, 5, 6, 7), silu=False)
                                    if xgen is not None:
                                        for _ in range(2):
                                            if next(xgen, "done") == "done":
                                                xgen = None
                                                break
                                    if i >= 1:
                                        st_D(i - 1)
                                    if i + 3 < 8:
                                        st_seg(i + 3)
                                if xgen is not None:
                                    for _ in xgen:
                                        pass
                                st_D(7)
                                if g == 1:
                                    dump("YT", YT[:, :, :], ["YT0", "YT1"])
                                    dump("E3g", E3[:, :, :], ["E3"])
                                    dump("cbmg", cbm[:, :, :], ["cbm0", "cbm1"])
                                k.mark(f"L1h{hf}_g{g}_outproj")
                                rv = ring_view(so, (4, 1024))
                                for tt in range(8):
                                    for ch in range(2):
                                        pi = nps()
                                        for fb in range(4):
                                            k.mm(psf(pi), YT[:, fb, tt * 128:(tt + 1) * 128],
                                                 rv[:, fb, ch * 512:(ch + 1) * 512], fb == 0, fb == 3,
                                                 reads=[f"ring{so}", f"YT{tt // 4}"], writes=[f"ps{pi}"])
                                        hacc(hf * 8 + tt, ch, pi)
                                if g < 3:
                                    pso = issue_out_load(g + 1)
                            if hf == 1:
                                sq_tiles(list(range(8, 16)))
                            k.barrier()

        def ffn(i):
            k.mark(f"ffn{i}_norm")
            with ExitStack() as st:
                fT = sb("fT", [128, 8, 2048], BF16, st)
                hid = sb("hid", [128, 2, 4, 512], BF16, st)
                rl = sb("rl", [128, 2, 512], F32, st)
                norm_T(lambda j: h[:, j, :], lambda j: f"h{j}", 16, CV_GFFN + 8 * i, fT, lambda j: f"fT{j}",
                       cols=[presq.pop(j) for j in range(16)] if all(j in presq for j in range(16)) else None)
                FK = [f"fT{j}" for j in range(16)]
                hb = 0
                ri = 0
                k.mark(f"ffn{i}_body")
                units = {}

                def f_load(he):
                    a = load_unit(wcols(w_ffn1[i], he * 512))
                    b_ = load_unit(w_ffn2[i][he * 512:(he + 1) * 512, :].rearrange("(c p) n -> p c n", p=128),
                                   view=(4, 1024))
                    units[he] = (a, b_)

                def f_up(he, tq, hbuf):
                    s1 = units[he][0]
                    for fbl in range(4):
                        pi = nps()
                        for c in range(8):
                            k.mm(psf(pi), ring[s1][:, c, fbl * 128:(fbl + 1) * 128],
                                 fT[:, c, tq * 512:(tq + 1) * 512],
                                 c == 0, c == 7, reads=[f"ring{s1}"] + FK[tq * 4:tq * 4 + 4],
                                 writes=[f"ps{pi}"])
                        rb = state['rl'] % 2
                        state['rl'] += 1
                        k.ts('dve', rl[:, rb, :], psf(pi), 0.0, None, ALU.max, None, reads=[f"ps{pi}"],
                             writes=[f"rl{rb}"])
                        k.act(hid[:, hbuf, fbl, :], rl[:, rb, :], AF.Square, reads=[f"rl{rb}"],
                              writes=[f"hid{hbuf}_{fbl}"])

                def f_down(he, tq, hbuf):
                    s2 = units[he][1]
                    rv2 = ring_view(s2, (4, 1024))
                    for t4 in range(4):
                        for ch in range(2):
                            pi = nps()
                            for fbl in range(4):
                                k.mm(psf(pi), hid[:, hbuf, fbl, t4 * 128:(t4 + 1) * 128],
                                     rv2[:, fbl, ch * 512:(ch + 1) * 512], fbl == 0, fbl == 3,
                                     reads=[f"ring{s2}", f"hid{hbuf}_{fbl}"], writes=[f"ps{pi}"])
                            hacc(tq * 4 + t4, ch, pi)

                its = [(he, tq) for he in range(8) for tq in range(4)]
                f_load(0)
                f_up(0, 0, 0)
                for n_, (he, tq) in enumerate(its):
                    if n_ + 1 < len(its):
                        he2, tq2 = its[n_ + 1]
                        if he2 not in units:
                            f_load(he2)
                        f_up(he2, tq2, (n_ + 1) % 2)
                    f_down(he, tq, n_ % 2)
                    if he == 7:
                        sq_tiles(list(range(tq * 4, tq * 4 + 4)))
                k.barrier()

        for i in range(n_layers):
            if 'mem' in phases:
                phase_mem(i)
            if i == 0:
                if 'mix' in phases:
                    layer0_mixer()
            else:
                if not skip_l1_mixer:
                    layer1_mixer()
            if 'ffn' in phases:
                ffn(i)

        k.mark("final")
        gfin = sb("gfin", [128, 1024], F32)
        k.dma('sp', out=gfin[:, :], in_=gfin_d[:, :], sem="gf", writes=["gfin"])
        if all(tt in presq for tt in range(NT)):
            fcols = [presq.pop(tt) for tt in range(NT)]
        else:
            c0 = state['col']
            state['col'] += NT
            for tt in range(NT):
                k.act(junk[:, :], h[:, tt, :], AF.Square, reads=[f"h{tt}", "ssq"], writes=["junk", f"ssq{c0 + tt}"],
                      accum_out=ssq[:, c0 + tt:c0 + tt + 1])
            rkeys = [f"rstd{c0 + j}" for j in range(NT)]
            k.act(rstd[:, c0:c0 + NT], ssq[:, c0:c0 + NT], AF.Ln, reads=[f"ssq{c0 + j}" for j in range(NT)],
                  writes=rkeys, scale=1.0 / D, bias=EPS)
            k.act(rstd[:, c0:c0 + NT], rstd[:, c0:c0 + NT], AF.Exp, reads=rkeys, writes=rkeys, scale=-0.5)
            fcols = [c0 + tt for tt in range(NT)]
        for tt in range(NT):
            col = fcols[tt]
            k.stt('dve', h[:, tt, :], h[:, tt, :], rstd[:, col:col + 1], gfin[:, :],
                  ALU.mult, ALU.mult, reads=[f"h{tt}", f"rstd{col}", "gfin"], writes=[f"h{tt}"])
            k.dma('sp', out=out[tt * 128:(tt + 1) * 128, :], in_=h[:, tt, :], sem="out", reads=[f"h{tt}"])
        nc.sync.wait_ge(k.sem["out"], k.cnt["out"])
        if "dbg" in k.sem:
            nc.sync.wait_ge(k.sem["dbg"], k.cnt["dbg"])
        global MARKS
        MARKS = k.marks + [("end", k.npe)]
        print("build: waits", k.nwait, "counts", {e: k.cnt[e] for e in ['pe', 'act', 'dve', 'pool']})
    return nc


def host_consts(inp):
    f = np.float32
    cv = np.zeros((128, NCV), f)

    def pp(vec):
        return np.ascontiguousarray(np.asarray(vec, f).reshape(-1, 128).T)

    for i in range(2):
        cv[:, CV_GMIX + 8 * i:CV_GMIX + 8 * i + 8] = pp(inp["norm_mix"][i])
        cv[:, CV_GFFN + 8 * i:CV_GFFN + 8 * i + 8] = pp(inp["norm_ffn"][i])
        cv[:, CV_GMEM + 8 * i:CV_GMEM + 8 * i + 8] = pp(inp["mem_norm"][i])
    cv[:, CV_LNG:CV_LNG + 16] = pp(inp["a_ln_g"][0])
    cv[:, CV_GNORM:CV_GNORM + 16] = pp(inp["b_gnorm"][0])
    cv[:, CV_CONVB:CV_CONVB + 24] = pp(inp["b_conv_b"][0])
    cw = np.asarray(inp["b_conv_w"][0], f)
    cv[:, CV_CONVW:CV_CONVW + 96] = cw.reshape(4, 24, 128).transpose(2, 1, 0).reshape(128, 96)
    cv[:, CV_DCOL:CV_DCOL + 16] = pp(np.repeat(np.asarray(inp["b_d"][0], f), 64))
    bc = np.zeros((128, NBC), f)
    gfin = np.ascontiguousarray(np.broadcast_to(np.asarray(inp["final_norm"], f)[None, :], (128, 1024)))
    bc[:, BC_D:BC_D + 32] = np.asarray(inp["b_d"][0], f)[None, :]
    bc[:, BC_DTB:BC_DTB + 32] = np.asarray(inp["b_dt_bias"][0], f)[None, :]
    bc[:, BC_ALOG:BC_ALOG + 32] = np.asarray(inp["b_a_log"][0], f)[None, :]
    lnb = np.ascontiguousarray(np.broadcast_to(np.asarray(inp["a_ln_b"][0], f)[None, :], (128, 2048)))
    wsT = np.ascontiguousarray(np.asarray(inp["a_ws"][0], f).transpose(2, 0, 1))
    bsr = np.ascontiguousarray(np.asarray(inp["a_bs"][0], f).reshape(1, 1024))
    cst = np.zeros((128, 4, 128), f)
    cst[:, 0, :] = np.eye(128, dtype=f)
    cst[:, 1, :] = np.triu(np.ones((128, 128), f))
    cst[:, 2, :] = np.tril(np.ones((128, 128), f), -1)
    cst[:, 3, :] = 1.0
    return dict(cv=cv, bc=bc, wsT=wsT, bsr=bsr, cst=cst, lnb=lnb, gfin=gfin)


_CACHE = {}


def kernel(**inputs):
    inp = {k_: np.asarray(v) for k_, v in inputs.items()}
    hc = host_consts(inp)
    if "nc" not in _CACHE:
        _CACHE["nc"] = build_program()
    nc = _CACHE["nc"]
    shared = dict(
        w_kv=np.ascontiguousarray(inp["w_kv"], dtype=np.float32),
        w_out=np.ascontiguousarray(inp["w_out"], dtype=np.float32),
        w_ffn1=np.ascontiguousarray(inp["w_ffn1"], dtype=np.float32),
        w_ffn2=np.ascontiguousarray(inp["w_ffn2"], dtype=np.float32),
        a_in=np.ascontiguousarray(inp["a_in"][0], dtype=np.float32),
        b_in=np.ascontiguousarray(inp["b_in"][0], dtype=np.float32),
        **hc,
    )
    in_maps = []
    for b in range(8):
        m = dict(shared)
        m["x"] = np.ascontiguousarray(inp["x"][b], dtype=np.float32)
        m["mem"] = np.ascontiguousarray(inp["mem"][b], dtype=np.float32)
        in_maps.append(m)
    res = run_bass_kernel_spmd(nc, in_maps, core_ids=list(range(8)))
    return np.stack([np.asarray(r["out"], dtype=np.float32) for r in res.results], axis=0)
```

```python
import numpy as np
from contextlib import ExitStack
import concourse.bass as bass
import concourse.mybir as mybir
from concourse.bass_utils import run_bass_kernel_spmd

F32 = mybir.dt.float32
BF16 = mybir.dt.bfloat16
FP16 = mybir.dt.float16
AF = mybir.ActivationFunctionType
ALU = mybir.AluOpType
AX = mybir.AxisListType

T = 2048
D = 1024
NT = 16
EPS = 1e-6
NSLOT = 4
STOP = 0
MARKS = []
DEBUG = False

CV_GMIX, CV_GFFN, CV_GMEM, CV_LNG, CV_GNORM, CV_CONVB, CV_CONVW, CV_DCOL, NCV = 0, 16, 32, 48, 64, 80, 104, 200, 216
BC_D, BC_DTB, BC_ALOG, NBC = 0, 32, 64, 96


class KB:
    def __init__(self, nc, es):
        self.nc = nc
        self.es = es
        self.eng = {'pe': nc.tensor, 'act': nc.scalar, 'dve': nc.vector, 'pool': nc.gpsimd, 'sp': nc.sync}
        self.sem = {}
        self.cnt = {}
        for e in ['pe', 'act', 'dve', 'pool']:
            self.sem[e] = es.enter_context(nc.semaphore('s_' + e))
            self.cnt[e] = 0
        self.obs = {e: {} for e in self.eng}
        self.lw = {}
        self.rd = {}
        self.nwait = 0
        self.npe = 0
        self.marks = []

    def dma_sem(self, name):
        if name not in self.sem:
            self.sem[name] = self.es.enter_context(self.nc.semaphore('d_' + name))
            self.cnt[name] = 0
        return name

    def _deps(self, e, reads, writes):
        deps = {}
        for k in reads:
            m = self.lw.get(k)
            if m is not None:
                deps[m[0]] = max(deps.get(m[0], 0), m[1])
        for k in writes:
            m = self.lw.get(k)
            if m is not None:
                deps[m[0]] = max(deps.get(m[0], 0), m[1])
            for s, v in self.rd.get(k, {}).items():
                deps[s] = max(deps.get(s, 0), v)
        for s, v in deps.items():
            if s == e and e == 'pe':
                continue
            if self.obs[e].get(s, 0) >= v:
                continue
            assert v <= self.cnt[s], f"dep on pending signal {s}:{v} > {self.cnt[s]}"
            self.eng[e].wait_ge(self.sem[s], v)
            self.obs[e][s] = v
            self.nwait += 1

    def op(self, e, fn, reads=(), writes=(), signal=True):
        self._deps(e, reads, writes)
        ins = fn()
        mark = (e, self.cnt[e] + 1)
        if signal:
            ins.then_inc(self.sem[e], 1)
            self.cnt[e] += 1
        for k in writes:
            self.lw[k] = mark
            self.rd[k] = {}
        for k in reads:
            d = self.rd.setdefault(k, {})
            d[e] = max(d.get(e, 0), mark[1])
        return ins

    def dma(self, q, out, in_, sem, reads=(), writes=()):
        self.dma_sem(sem)
        self._deps(q, reads, writes)
        self.eng[q].dma_start(out=out, in_=in_).then_inc(self.sem[sem], 16)
        self.cnt[sem] += 16
        mark = (sem, self.cnt[sem])
        for k in writes:
            self.lw[k] = mark
            self.rd[k] = {}
        for k in reads:
            d = self.rd.setdefault(k, {})
            d[sem] = max(d.get(sem, 0), mark[1])

    def barrier(self):
        for e in self.eng:
            if e == 'pool':
                continue
            for s in list(self.sem.keys()):
                if s == e:
                    continue
                v = self.cnt[s]
                if v > 0 and self.obs[e].get(s, 0) < v:
                    self.eng[e].wait_ge(self.sem[s], v)
                    self.obs[e][s] = v

    def mark(self, name):
        self.marks.append((name, self.npe))

    def mm(self, out, lhsT, rhs, start, stop, reads, writes):
        self.npe += 2 if lhsT.dtype == F32 else 1
        return self.op('pe', lambda: self.nc.tensor.matmul(out, lhsT, rhs, start=start, stop=stop),
                       reads=reads, writes=writes, signal=stop)

    def tr(self, out, in_, ident, reads, writes, signal):
        self.npe += 1
        return self.op('pe', lambda: self.nc.tensor.transpose(out, in_, ident),
                       reads=reads, writes=writes, signal=signal)

    def act(self, out, in_, func, reads, writes, bias=None, scale=None, accum_out=None):
        kw = {}
        if bias is not None:
            kw['bias'] = bias
        if scale is not None:
            kw['scale'] = scale
        if accum_out is not None:
            kw['accum_out'] = accum_out
        return self.op('act', lambda: self.nc.scalar.activation(out, in_, func, **kw), reads=reads, writes=writes)

    def tt(self, e, out, in0, in1, op, reads, writes):
        return self.op(e, lambda: self.eng[e].tensor_tensor(out, in0, in1, op), reads=reads, writes=writes)

    def ts(self, e, out, in0, s1, s2, op0, op1, reads, writes):
        if op1 is None:
            s2, op1 = 0.0, ALU.add
        return self.op(e, lambda: self.eng[e].tensor_scalar(out, in0, s1, s2, op0, op1), reads=reads, writes=writes)

    def stt(self, e, out, in0, scalar, in1, op0, op1, reads, writes):
        return self.op(e, lambda: self.eng[e].scalar_tensor_tensor(out, in0, scalar, in1, op0, op1),
                       reads=reads, writes=writes)

    def copy(self, e, out, in_, reads, writes):
        if e == 'act':
            return self.op('act', lambda: self.nc.scalar.copy(out, in_), reads=reads, writes=writes)
        return self.op(e, lambda: self.eng[e].tensor_copy(out, in_), reads=reads, writes=writes)


def bcast(ap, axis, n):
    a = ap.unsqueeze(axis)
    shp = list(a.shape)
    shp[axis] = n
    return a.broadcast_to(shp)


def build_program(dbg=None, skip_l1_mixer=False, n_layers=2, phases=('mem', 'mix', 'ffn')):
    nc = bass.Bass("TRN2", target_bir_lowering=False)
    dr = {}

    def din(name, shape):
        dr[name] = nc.dram_tensor(name, list(shape), F32, kind="ExternalInput").ap()
        return dr[name]

    x = din("x", [T, D])
    mem = din("mem", [256, D])
    w_kv = din("w_kv", [2, 1024, 2048])
    w_out = din("w_out", [2, 3072, 1024])
    w_ffn1 = din("w_ffn1", [2, 1024, 4096])
    w_ffn2 = din("w_ffn2", [2, 4096, 1024])
    a_in = din("a_in", [1024, 5120])
    b_in = din("b_in", [1024, 6176])
    cv_d = din("cv", [128, NCV])
    bc_d = din("bc", [128, NBC])
    wsT_d = din("wsT", [128, 8, 128])
    bsr_d = din("bsr", [1, 1024])
    cst_d = din("cst", [128, 4, 128])
    lnb_d = din("lnb", [128, 2048])
    gfin_d = din("gfin", [128, 1024])
    out = nc.dram_tensor("out", [T, D], F32, kind="ExternalOutput").ap()
    dbg_out = None
    if dbg is not None:
        dbg_out = nc.dram_tensor("dbg", list(dbg[1]), F32, kind="ExternalOutput").ap()

    with ExitStack() as es:
        k = KB(nc, es)

        uid = [0]
        dumped = set()

        def dump(tag, ap, reads):
            if not DEBUG or tag in dumped:
                return
            dumped.add(tag)
            dt_ = nc.dram_tensor("dbg_" + tag, list(ap.shape), ap.dtype, kind="ExternalOutput").ap()
            k.dma('sp', out=dt_, in_=ap, sem="dbg", reads=reads)

        def sb(name, shape, dt, stack=es):
            uid[0] += 1
            return stack.enter_context(nc.sbuf_tensor(f"sb_{name}_{uid[0]}", list(shape), dt))

        h = sb("h", [128, NT, D], F32)
        ring = [sb(f"ring{i}", [128, 8, 512], BF16) for i in range(NSLOT)]
        cv = sb("cv", [128, NCV], F32)
        bc = sb("bc", [128, NBC], F32)
        cstf = sb("cstf", [128, 4, 128], F32)
        cstb = sb("cstb", [128, 4, 128], BF16)
        kT = sb("kT", [128, 8, 256], BF16)
        vmem = sb("vmem", [128, 2, 1024], BF16)
        wdt = sb("wdt", [128, 8, 32], BF16)
        ssq = sb("ssq", [128, 128], F32)
        rstd = sb("rstd", [128, 128], F32)
        ybf = [sb(f"ybf{i}", [128, 1024], BF16) for i in range(2)]
        junk = sb("junk", [128, 1024], BF16)
        ps = [es.enter_context(nc.psum_tensor(f"ps{i}", [128, 512], F32)) for i in range(8)]
        ident_b = cstb[:, 0, :]
        triu_b = cstb[:, 1, :]
        ones_b = cstb[:, 3, :]
        gtl_b = cstb[:, 2, :]
        triu_f = cstf[:, 1, :]
        gtl_f = cstf[:, 2, :]
        ones_f = cstf[:, 3, :]

        state = {'ps': 0, 'slot': 0, 'col': 0, 'yb': 0, 'sv': 0, 'rl': 0}

        def nps(pool=(0, 1, 2, 3, 4, 5, 6, 7)):
            i = pool[state['ps'] % len(pool)]
            state['ps'] += 1
            return i

        def psf(i):
            return ps[i][:, :]

        def psb(i):
            return ps[i][:, :].bitcast(BF16)

        def load_unit(src_ap, view=None):
            s = state['slot'] % NSLOT
            state['slot'] += 1
            dst = ring[s][:, :, :] if view is None else ring_view(s, view)
            k.dma('pool', out=dst, in_=src_ap, sem=f"r{s}", writes=[f"ring{s}"])
            return s

        def ring_view(s, view):
            a, b = view
            return ring[s][:, :, :].rearrange("p c n -> p (c n)").rearrange("p (a b) -> p a b", a=a)

        def wcols(w2d, c0, ncols=512):
            return w2d[:, c0:c0 + ncols].rearrange("(c p) n -> p c n", p=128)

        k.dma('sp', out=cv[:, :], in_=cv_d[:, :], sem="c0", writes=["cv"])
        k.dma('sp', out=bc[:, :], in_=bc_d[:, :], sem="c0", writes=["bc"])
        k.dma('sp', out=cstf[:, :, :], in_=cst_d[:, :, :], sem="c0", writes=["cstf"])
        k.dma('pool', out=cstb[:, :, :], in_=cst_d[:, :, :], sem="c1", writes=["cstb"])
        for key in ["cv", "bc", "cstf"]:
            k.lw[key] = ("c0", k.cnt["c0"])
        k.dma('pool', out=wdt[:, :, :], in_=b_in[:, 5120:5152].rearrange("(c p) n -> p c n", p=128),
              sem="wd", writes=["wdt"])
        x_v = x.rearrange("(t p) d -> p t d", p=128)

        def load_x(gs=(0, 1, 2, 3)):
            for g in gs:
                k.dma('sp', out=h[:, 4 * g:4 * g + 4, :], in_=x_v[:, 4 * g:4 * g + 4, :], sem=f"x{g}",
                      writes=[f"h{t}" for t in range(4 * g, 4 * g + 4)])

        k.op('dve', lambda: nc.vector.memset(ssq[:, :], 0.0), writes=["ssq"])

        presq = {}

        def sq_tiles(tiles):
            n = len(tiles)
            c0 = state['col']
            state['col'] += n
            for j, tt in enumerate(tiles):
                k.act(junk[:, :], h[:, tt, :], AF.Square, reads=[f"h{tt}", "ssq"], writes=["junk", f"ssq{c0 + j}"],
                      accum_out=ssq[:, c0 + j:c0 + j + 1])
            rkeys = [f"rstd{c0 + j}" for j in range(n)]
            k.act(rstd[:, c0:c0 + n], ssq[:, c0:c0 + n], AF.Ln, reads=[f"ssq{c0 + j}" for j in range(n)],
                  writes=rkeys, scale=1.0 / D, bias=EPS)
            k.act(rstd[:, c0:c0 + n], rstd[:, c0:c0 + n], AF.Exp, reads=rkeys, writes=rkeys, scale=-0.5)
            for j, tt in enumerate(tiles):
                presq[tt] = c0 + j

        def norm_T(src_fn, src_key_fn, ntiles, gcol, dst, dst_key_fn, dcols=D, cols=None):
            if cols is None:
                c0 = state['col']
                state['col'] += ntiles
                for j in range(ntiles):
                    k.act(junk[:, :], src_fn(j), AF.Square, reads=[src_key_fn(j), "ssq"],
                          writes=["junk", f"ssq{c0 + j}"], accum_out=ssq[:, c0 + j:c0 + j + 1])
                rkeys = [f"rstd{c0 + j}" for j in range(ntiles)]
                k.act(rstd[:, c0:c0 + ntiles], ssq[:, c0:c0 + ntiles], AF.Ln,
                      reads=[f"ssq{c0 + j}" for j in range(ntiles)], writes=rkeys, scale=1.0 / dcols, bias=EPS)
                k.act(rstd[:, c0:c0 + ntiles], rstd[:, c0:c0 + ntiles], AF.Exp, reads=rkeys, writes=rkeys, scale=-0.5)
                cols = [c0 + j for j in range(ntiles)]

            def stage1(j):
                col = cols[j]
                yb = state['yb'] % 2
                state['yb'] += 1
                k.ts('dve', ybf[yb][:, :], src_fn(j), rstd[:, col:col + 1], None, ALU.mult, None,
                     reads=[src_key_fn(j), f"rstd{col}"], writes=[f"ybf{yb}"])
                pi = nps()
                pv = psb(pi).rearrange("p (c n) -> p c n", c=8)
                for c in range(8):
                    k.tr(pv[:, c, :], ybf[yb][:, c * 128:(c + 1) * 128], ident_b,
                         reads=[f"ybf{yb}", "cstb"], writes=[f"ps{pi}"], signal=(c == 7))
                return pi, pv

            def stage2(j, pi, pv):
                k.tt('dve', dst[:, :, j * 128:(j + 1) * 128], pv,
                     bcast(cv[:, gcol:gcol + 8], 2, 128), ALU.mult,
                     reads=[f"ps{pi}", "cv"], writes=[dst_key_fn(j)])

            prev = stage1(0)
            for j in range(ntiles):
                nxt = stage1(j + 1) if j + 1 < ntiles else None
                stage2(j, *prev)
                prev = nxt

        def hacc(tt, ch, pi):
            k.tt('dve', h[:, tt, ch * 512:(ch + 1) * 512], h[:, tt, ch * 512:(ch + 1) * 512], psf(pi), ALU.add,
                 reads=[f"ps{pi}", f"h{tt}"], writes=[f"h{tt}"])

        def phase_mem(i):
            k.mark(f"mem{i}")
            with ExitStack() as st:
                memt = sb("memt", [128, 2, 1024], F32, st)
                mT = sb("mT", [128, 8, 256], BF16, st)
                k.dma('sp', out=memt[:, :, :], in_=mem.rearrange("(t p) d -> p t d", p=128), sem="mm",
                      writes=["memt0", "memt1"])
                if i == 0:
                    load_x((0, 1))
                norm_T(lambda j: memt[:, j, :], lambda j: f"memt{j}", 2, CV_GMEM + 8 * i, mT,
                       lambda j: f"mT{j}")
                for ku in range(2):
                    s = load_unit(wcols(w_kv[i], ku * 512))
                    for fb in range(4):
                        pi = nps()
                        for c in range(8):
                            k.mm(ps[pi][:, 0:256], ring[s][:, c, fb * 128:(fb + 1) * 128], mT[:, c, :],
                                 c == 0, c == 7, reads=[f"ring{s}", "mT0", "mT1"], writes=[f"ps{pi}"])
                        k.copy('act', kT[:, ku * 4 + fb, :], ps[pi][:, 0:256], reads=[f"ps{pi}"],
                               writes=[f"kT{ku * 4 + fb}"])
                for vu in range(2):
                    s = load_unit(wcols(w_kv[i], 1024 + vu * 512))
                    for mt in range(2):
                        pi = nps()
                        for c in range(8):
                            k.mm(psf(pi), mT[:, c, mt * 128:(mt + 1) * 128], ring[s][:, c, :],
                                 c == 0, c == 7, reads=[f"ring{s}", f"mT{mt}"], writes=[f"ps{pi}"])
                        k.copy('act', vmem[:, mt, vu * 512:(vu + 1) * 512], psf(pi), reads=[f"ps{pi}"],
                               writes=[f"vmem{mt}_{vu}"])
                k.barrier()

        VM_KEYS = [f"vmem{mt}_{vu}" for mt in range(2) for vu in range(2)]

        def attention(i, hf, xT, qcol_w, qcol0, st):
            qT = sb("qT", [128, 2, 2, 1024], BF16, st)
            moT = sb("moT", [128, 8, 1024], BF16, st)
            PT = sb("PT", [128, 2, 2, 512], BF16, st)
            rden = sb("rden", [128, 2, 512], F32, st)
            XK = [f"xT{j}" for j in range(8)]
            pcount = 0
            for qu in range(2):
                s = load_unit(wcols(qcol_w, qcol0 + qu * 512))
                for hl in range(2):
                    hh = 2 * qu + hl
                    qb = hh % 2
                    for fb in range(2):
                        for nh in range(2):
                            pi = nps()
                            for c in range(8):
                                k.mm(psf(pi), ring[s][:, c, hl * 256 + fb * 128: hl * 256 + (fb + 1) * 128],
                                     xT[:, c, nh * 512:(nh + 1) * 512], c == 0, c == 7,
                                     reads=[f"ring{s}"] + XK[nh * 4:nh * 4 + 4], writes=[f"ps{pi}"])
                            k.copy('act', qT[:, qb, fb, nh * 512:(nh + 1) * 512], psf(pi), reads=[f"ps{pi}"],
                                   writes=[f"qT{qb}_{fb}_{nh}"])
                    for nh in range(2):
                        pb = pcount % 2
                        pcount += 1
                        for mb in range(2):
                            pi = nps()
                            for fb in range(2):
                                k.mm(psf(pi), kT[:, hh * 2 + fb, mb * 128:(mb + 1) * 128],
                                     qT[:, qb, fb, nh * 512:(nh + 1) * 512], fb == 0, fb == 1,
                                     reads=[f"kT{hh * 2 + fb}", f"qT{qb}_{fb}_{nh}"], writes=[f"ps{pi}"])
                            k.act(PT[:, pb, mb, :], psf(pi), AF.Exp, reads=[f"ps{pi}"], writes=[f"PT{pb}_{mb}"],
                                  scale=1.0 / 16.0)
                        pi = nps()
                        for mb in range(2):
                            k.mm(psf(pi), ones_b, PT[:, pb, mb, :], mb == 0, mb == 1,
                                 reads=["cstb", f"PT{pb}_{mb}"], writes=[f"ps{pi}"])
                        k.op('dve', lambda: nc.vector.reciprocal(rden[:, pb, :], psf(pi)), reads=[f"ps{pi}"],
                             writes=[f"rden{pb}"])
                        for fb in range(2):
                            pi = nps()
                            for mb in range(2):
                                k.mm(psf(pi), vmem[:, mb, hh * 256 + fb * 128: hh * 256 + (fb + 1) * 128],
                                     PT[:, pb, mb, :], mb == 0, mb == 1,
                                     reads=VM_KEYS + [f"PT{pb}_{mb}"], writes=[f"ps{pi}"])
                            k.tt('dve', moT[:, hh * 2 + fb, nh * 512:(nh + 1) * 512], psf(pi), rden[:, pb, :],
                                 ALU.mult, reads=[f"ps{pi}", f"rden{pb}"], writes=[f"moT{hh * 2 + fb}_{nh}"])
            su = [load_unit(w_out[i][2048:3072, ch * 512:(ch + 1) * 512].rearrange("(c p) n -> p c n", p=128))
                  for ch in range(2)]
            for tt in range(8):
                for ch in range(2):
                    pi = nps()
                    for c in range(8):
                        k.mm(psf(pi), moT[:, c, tt * 128:(tt + 1) * 128], ring[su[ch]][:, c, :], c == 0, c == 7,
                             reads=[f"ring{su[ch]}", f"moT{c}_{tt // 4}"], writes=[f"ps{pi}"])
                    hacc(hf * 8 + tt, ch, pi)

        def layer0_mixer():
            with ExitStack() as L:
                WT = sb("WT", [128, 8, 128], BF16, L)
                Cst = sb("Cst", [128, 16, 128], F32, L)
                with ExitStack() as st:
                    wsf = sb("wsf", [128, 8, 128], F32, st)
                    lnb = sb("lnb", [128, 2048], BF16, st)
                    bsr = sb("bsr", [1, 1024], BF16, st)
                    bsrf = sb("bsrf", [1, 1024], F32, st)
                    k.dma('sp', out=wsf[:, :, :], in_=wsT_d[:, :, :], sem="m0", writes=["wsf"])
                    k.dma('sp', out=bsrf[:, :], in_=bsr_d[:, :], sem="m1", writes=["bsrf"])
                    k.op('dve', lambda: nc.vector.tensor_copy(bsr[:, :], bsrf[:, :]), reads=["bsrf"], writes=["bsr"])
                    lnbf = sb("lnbf", [128, 2048], F32, st)
                    k.dma('sp', out=lnbf[:, :], in_=lnb_d[:, :], sem="m2", writes=["lnbf"])
                    load_x((2, 3))
                    k.op('dve', lambda: nc.vector.tensor_copy(lnb[:, :], lnbf[:, :]),
                         reads=["lnbf"], writes=["lnb"])
                    k.tt('dve', WT[:, :, :], wsf[:, :, :], bcast(triu_f, 1, 8), ALU.mult,
                         reads=["wsf", "cstf"], writes=["WT"])
                    for db in range(16):
                        g = db // 2
                        pi = nps()
                        k.mm(ps[pi][:, 0:128], lnb[:, db * 128:(db + 1) * 128], WT[:, g, :], True, False,
                             reads=["lnb", "WT"], writes=[f"ps{pi}"])
                        k.mm(ps[pi][:, 0:128], ones_b[0:1, :], bsr[0:1, g * 128:(g + 1) * 128], False, True,
                             reads=["cstb", "bsr"], writes=[f"ps{pi}"])
                        k.copy('act', Cst[:, db, :], ps[pi][:, 0:128], reads=[f"ps{pi}"], writes=["Cst"])
                    k.barrier()
                if STOP == 1:
                    return
                for hf in range(2):
                    with ExitStack() as H:
                        k.mark(f"L0h{hf}_norm")
                        xT = sb("xT", [128, 8, 1024], BF16, H)
                        norm_T(lambda j: h[:, hf * 8 + j, :], lambda j: f"h{hf * 8 + j}", 8, CV_GMIX, xT,
                               lambda j: f"xT{j}", cols=[presq.pop(hf * 8 + j) for j in range(8)]
                               if all((hf * 8 + j) in presq for j in range(8)) else None)
                        if hf == 0:
                            sq_tiles(list(range(8, 16)))
                        else:
                            sq_tiles(list(range(0, 8)))
                        XK = [f"xT{j}" for j in range(8)]
                        if STOP == 2:
                            return
                        with ExitStack() as st:
                            k.mark(f"L0h{hf}_attn")
                            attention(0, hf, xT, a_in, 4096, st)
                            k.barrier()
                        if STOP == 3:
                            return
                        with ExitStack() as st:
                            V = sb("V", [128, 8, 2048], BF16, st)
                            U = sb("U", [128, 2, 4, 1024], BF16, st)
                            svt = sb("svt", [128, 2, 512], F32, st)
                            stats = sb("stats", [128, 8, 4, 6], F32, st)
                            mv = sb("mv", [128, 8, 2], F32, st)
                            lr = sb("lr", [128, 8, 2], F32, st)
                            k.mark(f"L0h{hf}_vproj")
                            for vu in range(4):
                                s = load_unit(wcols(a_in, 2048 + vu * 512))
                                for tt in range(8):
                                    pi = nps()
                                    for c in range(8):
                                        k.mm(psf(pi), xT[:, c, tt * 128:(tt + 1) * 128], ring[s][:, c, :],
                                             c == 0, c == 7, reads=[f"ring{s}", f"xT{tt}"], writes=[f"ps{pi}"])
                                    k.act(V[:, tt, vu * 512:(vu + 1) * 512], psf(pi), AF.Gelu, reads=[f"ps{pi}"],
                                          writes=[f"V{tt}"])
                            if STOP == 4:
                                return
                            for tt in range(8):
                                for q4 in range(4):
                                    k.op('dve', lambda: nc.vector.bn_stats(stats[:, tt, q4, :],
                                                                           V[:, tt, q4 * 512:(q4 + 1) * 512]),
                                         reads=[f"V{tt}"], writes=[f"stats{tt}"])
                                k.op('dve', lambda: nc.vector.bn_aggr(mv[:, tt, :], stats[:, tt, :, :]),
                                     reads=[f"stats{tt}"], writes=[f"mv{tt}"])
                            k.act(lr[:, :, 0], mv[:, :, 1], AF.Ln, reads=[f"mv{tt}" for tt in range(8)],
                                  writes=[f"lr{tt}" for tt in range(8)], bias=EPS)
                            k.act(lr[:, :, 0], lr[:, :, 0], AF.Exp, reads=[f"lr{tt}" for tt in range(8)],
                                  writes=[f"lr{tt}" for tt in range(8)], scale=-0.5)
                            for tt in range(8):
                                k.stt('dve', lr[:, tt, 1:2], mv[:, tt, 0:1], -1.0, lr[:, tt, 0:1], ALU.mult,
                                      ALU.mult, reads=[f"mv{tt}", f"lr{tt}"], writes=[f"lr{tt}"])
                                k.ts('dve', V[:, tt, :], V[:, tt, :], lr[:, tt, 0:1], lr[:, tt, 1:2], ALU.mult,
                                     ALU.add, reads=[f"V{tt}", f"lr{tt}"], writes=[f"V{tt}"])
                            if STOP == 5:
                                return
                            gl = {}

                            def g_loads(gp):
                                a = load_unit(wcols(a_in, gp * 512))
                                b_ = load_unit(w_out[0][gp * 512:(gp + 1) * 512, :].rearrange(
                                    "(c p) n -> p c n", p=128), view=(4, 1024))
                                gl[gp] = (a, b_)

                            def g_uproj(gp):
                                k.mark(f"L0h{hf}_gp{gp}")
                                ub = gp % 2
                                s_ = gl[gp][0]
                                for fb in range(4):
                                    for nh in range(2):
                                        pi = nps()
                                        for c in range(8):
                                            k.mm(psf(pi), ring[s_][:, c, fb * 128:(fb + 1) * 128],
                                                 xT[:, c, nh * 512:(nh + 1) * 512], c == 0, c == 7,
                                                 reads=[f"ring{s_}"] + XK[nh * 4:nh * 4 + 4], writes=[f"ps{pi}"])
                                        k.act(U[:, ub, fb, nh * 512:(nh + 1) * 512], psf(pi), AF.Gelu,
                                              reads=[f"ps{pi}"], writes=[f"U{ub}_{fb}_{nh}"])

                            def g_gate(gp):
                                ub = gp % 2
                                for fb in range(4):
                                    db = gp * 4 + fb
                                    g = db // 2
                                    for cq in range(2):
                                        pi = nps()
                                        for c4 in range(4):
                                            ch = cq * 4 + c4
                                            k.mm(ps[pi][:, c4 * 128:(c4 + 1) * 128],
                                                 V[:, ch, db * 128:(db + 1) * 128], WT[:, g, :], True, True,
                                                 reads=[f"V{ch}", "WT"], writes=[f"ps{pi}"])
                                        sb_i = state['sv'] % 2
                                        state['sv'] += 1
                                        k.act(svt[:, sb_i, :], psf(pi), AF.Identity, reads=[f"ps{pi}", "cv"],
                                              writes=[f"svt{sb_i}"], scale=cv[:, CV_LNG + db:CV_LNG + db + 1])
                                        k.tt('dve', svt[:, sb_i, :].rearrange("p (a b) -> p a b", a=4),
                                             svt[:, sb_i, :].rearrange("p (a b) -> p a b", a=4),
                                             bcast(Cst[:, db, :], 1, 4), ALU.add,
                                             reads=[f"svt{sb_i}", "Cst"], writes=[f"svt{sb_i}"])
                                        k.tt('dve', U[:, ub, fb, cq * 512:(cq + 1) * 512],
                                             U[:, ub, fb, cq * 512:(cq + 1) * 512], svt[:, sb_i, :], ALU.mult,
                                             reads=[f"svt{sb_i}", f"U{ub}_{fb}_{cq}"], writes=[f"U{ub}_{fb}_{cq}"])

                            def g_out(gp):
                                ub = gp % 2
                                so_ = gl[gp][1]
                                rv = ring_view(so_, (4, 1024))
                                for tt in range(8):
                                    for ch in range(2):
                                        pi = nps()
                                        for fb in range(4):
                                            k.mm(psf(pi), U[:, ub, fb, tt * 128:(tt + 1) * 128],
                                                 rv[:, fb, ch * 512:(ch + 1) * 512], fb == 0, fb == 3,
                                                 reads=[f"ring{so_}", f"U{ub}_{fb}_{tt // 4}"], writes=[f"ps{pi}"])
                                        hacc(hf * 8 + tt, ch, pi)

                            g_loads(0)
                            g_uproj(0)
                            g_gate(0)
                            for gp in range(1, 4):
                                g_loads(gp)
                                g_uproj(gp)
                                g_out(gp - 1)
                                g_gate(gp)
                            g_out(3)
                            if hf == 1:
                                sq_tiles(list(range(8, 16)))
                            k.barrier()


        def layer1_mixer():
            with ExitStack() as L:
                HAL = sb("HAL", [128, 24, 3], BF16, L)
                prevS = sb("prevS", [128, 4, 512], F32, L)
                A_b = sb("A_b", [128, 32], F32, L)
                gss = sb("gss", [128, 64], F32, L)
                grs = sb("grs", [128, 64], F32, L)
                k.op('dve', lambda: nc.vector.memset(HAL[:, :, :], 0.0), writes=["HAL"])
                k.op('dve', lambda: nc.vector.memset(prevS[:, :, :], 0.0), writes=[f"prevS{g}" for g in range(4)])
                k.op('dve', lambda: nc.vector.memset(gss[:, :], 0.0), writes=["gss"])
                k.act(A_b[:, :], bc[:, BC_ALOG:BC_ALOG + 32], AF.Exp, reads=["bc"], writes=["A_b"])
                k.ts('dve', A_b[:, :], A_b[:, :], -1.0, None, ALU.mult, None, reads=["A_b"], writes=["A_b"])
                gcol = [0]
                for hf in range(2):
                    with ExitStack() as H:
                        k.mark(f"L1h{hf}_norm")
                        xT = sb("xT", [128, 8, 1024], BF16, H)
                        norm_T(lambda j: h[:, hf * 8 + j, :], lambda j: f"h{hf * 8 + j}", 8, CV_GMIX + 8, xT,
                               lambda j: f"xT{j}", cols=[presq.pop(hf * 8 + j) for j in range(8)]
                               if all((hf * 8 + j) in presq for j in range(8)) else None)
                        if hf == 1:
                            sq_tiles(list(range(0, 8)))
                        XK = [f"xT{j}" for j in range(8)]
                        with ExitStack() as st:
                            k.mark(f"L1h{hf}_attn")
                            attention(1, hf, xT, b_in, 5152, st)
                            k.barrier()
                        with ExitStack() as st:
                            Zs = sb("Zs", [128, 8, 512], BF16, st)
                            XB = sb("XB", [128, 6, 1027], BF16, st)
                            XC = sb("XC", [128, 6, 1024], BF16, st)
                            DG = sb("DG", [128, 2, 4, 128], BF16, st)
                            dtsA = sb("dts", [128, 2, 8, 8], F32, st)
                            daA = sb("da", [128, 2, 8, 8], F32, st)
                            Rh = sb("Rh", [128, 2, 4, 128], FP16, st)
                            c16 = sb("c16", [128, 2, 128], FP16, st)
                            dahA = sb("dah", [128, 2, 8, 8], FP16, st)
                            E3 = sb("E3", [128, 3, 8, 8], F32, st)
                            cbm = sb("cbm", [128, 8, 128], BF16, st)
                            E = sb("E", [128, 3, 8, 128], BF16, st)
                            MT = E
                            xdt = sb("xdt", [128, 2, 512], BF16, st)
                            DGD = sb("DGD", [128, 4, 128], BF16, st)
                            xdd = sb("xdd", [128, 2, 512], BF16, st)
                            Bc = sb("Bc", [128, 2, 128], BF16, st)
                            yb = sb("yb", [128, 1, 512], F32, st)
                            ynb = sb("ynb", [128, 2, 512], BF16, st)
                            prevB = sb("prevB", [128, 2, 512], BF16, st)
                            YT = sb("YT", [128, 4, 1024], BF16, st)
                            k.copy('dve', c16[:, :, :], cstf[:, 1:3, :], reads=["cstf"], writes=["c16"])
                            for g in range(4):
                                k.mark(f"L1h{hf}_g{g}_proj")
                                def issue_proj_loads(gg):
                                    a = load_unit(wcols(b_in, gg * 512))
                                    b_ = load_unit(wcols(b_in, 2048 + gg * 512))
                                    c_ = state['slot'] % NSLOT
                                    state['slot'] += 1
                                    k.dma('pool', out=ring[c_][:, :, 0:128],
                                          in_=wcols(b_in, 4096 + gg * 128, 128), sem=f"r{c_}", writes=[f"ring{c_}"])
                                    k.dma('pool', out=ring[c_][:, :, 128:256],
                                          in_=wcols(b_in, 4608 + gg * 128, 128), sem=f"r{c_}", writes=[f"ring{c_}"])
                                    return a, b_, c_

                                def issue_out_load(gg):
                                    return load_unit(w_out[1][gg * 512:(gg + 1) * 512, :].rearrange(
                                        "(c p) n -> p c n", p=128), view=(4, 1024))

                                if g == 0:
                                    pl = issue_proj_loads(0)
                                    pso = issue_out_load(0)
                                sz, sx, s3 = pl
                                so = pso
                                gpar = g % 2
                                dts = dtsA[:, gpar, :, :]
                                da = daA[:, gpar, :, :]
                                dah = dahA[:, gpar, :, :]
                                KD, KA, KH = f"dts{gpar}", f"da{gpar}", f"dah{gpar}"

                                def dt_path(gg, pool=None):
                                    gp_ = gg % 2
                                    pi = nps(pool) if pool else nps()
                                    for tt in range(8):
                                        for c in range(8):
                                            k.mm(ps[pi][:, tt * 8:(tt + 1) * 8], xT[:, c, tt * 128:(tt + 1) * 128],
                                                 wdt[:, c, gg * 8:(gg + 1) * 8], c == 0, c == 7,
                                                 reads=["wdt", f"xT{tt}"], writes=[f"ps{pi}"])
                                    k.tt('dve', dtsA[:, gp_, :, :], ps[pi][:, 0:64].rearrange("p (a b) -> p a b", a=8),
                                         bcast(bc[:, BC_DTB + gg * 8:BC_DTB + gg * 8 + 8], 1, 8), ALU.add,
                                         reads=[f"ps{pi}", "bc"], writes=[f"dts{gp_}"])
                                    k.act(dtsA[:, gp_, :, :], dtsA[:, gp_, :, :], AF.Exp, reads=[f"dts{gp_}"],
                                          writes=[f"dts{gp_}"])
                                    k.act(dtsA[:, gp_, :, :], dtsA[:, gp_, :, :], AF.Ln, reads=[f"dts{gp_}"],
                                          writes=[f"dts{gp_}"], bias=1.0)
                                    k.tt('dve', daA[:, gp_, :, :], dtsA[:, gp_, :, :],
                                         bcast(A_b[:, gg * 8:(gg + 1) * 8], 1, 8), ALU.mult,
                                         reads=[f"dts{gp_}", "A_b"], writes=[f"da{gp_}"])
                                    k.copy('pool', dahA[:, gp_, :, :], daA[:, gp_, :, :], reads=[f"da{gp_}"],
                                           writes=[f"dah{gp_}"])

                                def z_tile(gg, tt, slot, pool=None, silu=True):
                                    pi = nps(pool) if pool else nps()
                                    for c in range(8):
                                        k.mm(psf(pi), xT[:, c, tt * 128:(tt + 1) * 128], ring[slot][:, c, :],
                                             c == 0, c == 7, reads=[f"ring{slot}", f"xT{tt}"], writes=[f"ps{pi}"])
                                    k.act(Zs[:, tt, :], psf(pi), AF.Silu if silu else AF.Identity, reads=[f"ps{pi}"],
                                          writes=[f"Zs{tt}"])

                                if g == 0:
                                    dt_path(0)
                                    for tt in range(8):
                                        z_tile(0, tt, sz)
                                else:
                                    for tt in range(8):
                                        k.act(Zs[:, tt, :], Zs[:, tt, :], AF.Silu, reads=[f"Zs{tt}"], writes=[f"Zs{tt}"])
                                blks = [g * 4 + fb for fb in range(4)] + [16 + g, 20 + g]

                                def xb_proj(gg, slots, pool=None):
                                    sx_, s3_ = slots
                                    bl = [gg * 4 + fb for fb in range(4)] + [16 + gg, 20 + gg]
                                    for fb in range(6):
                                        if fb < 4:
                                            wsl = lambda c: ring[sx_][:, c, fb * 128:(fb + 1) * 128]
                                            wk = f"ring{sx_}"
                                        else:
                                            wsl = lambda c: ring[s3_][:, c, (fb - 4) * 128:(fb - 3) * 128]
                                            wk = f"ring{s3_}"
                                        k.copy('dve', XB[:, fb, 0:3], HAL[:, bl[fb], :], reads=["HAL"],
                                               writes=[f"XB{fb}_0"])
                                        for nh in range(2):
                                            pi = nps(pool) if pool else nps()
                                            for c in range(8):
                                                k.mm(psf(pi), wsl(c), xT[:, c, nh * 512:(nh + 1) * 512], c == 0, c == 7,
                                                     reads=[wk] + XK[nh * 4:nh * 4 + 4], writes=[f"ps{pi}"])
                                            k.copy('dve', XB[:, fb, 3 + nh * 512:3 + (nh + 1) * 512], psf(pi),
                                                   reads=[f"ps{pi}"], writes=[f"XB{fb}_{nh}"])
                                            if nh == 1:
                                                k.copy('dve', HAL[:, bl[fb], :], XB[:, fb, 1024:1027],
                                                       reads=[f"XB{fb}_1"], writes=["HAL"])
                                            yield

                                if g == 0:
                                    for _ in xb_proj(0, (sx, s3)):
                                        pass
                                for fb in range(6):
                                    blk = blks[fb]
                                    dgi = fb % 2
                                    for kk in range(4):
                                        k.ts('dve', DG[:, dgi, kk, :], cstf[:, 0, :],
                                             cv[:, CV_CONVW + blk * 4 + kk:CV_CONVW + blk * 4 + kk + 1], None,
                                             ALU.mult, None, reads=["cstf", "cv"], writes=[f"DG{dgi}"])
                                    for nh in range(2):
                                        pi = nps()
                                        for kk in range(4):
                                            k.mm(psf(pi), DG[:, dgi, kk, :],
                                                 XB[:, fb, kk + nh * 512:kk + nh * 512 + 512], kk == 0, kk == 3,
                                                 reads=[f"DG{dgi}", f"XB{fb}_0", f"XB{fb}_1"] if nh else
                                                 [f"DG{dgi}", f"XB{fb}_0"] + [f"XB{fb}_0"],
                                                 writes=[f"ps{pi}"])
                                        k.act(XC[:, fb, nh * 512:(nh + 1) * 512], psf(pi), AF.Silu,
                                              reads=[f"ps{pi}", "cv"], writes=[f"XC{fb}_{nh}"],
                                              bias=cv[:, CV_CONVB + blk:CV_CONVB + blk + 1])
                                dump("Zs", Zs[:, :, :], [f"Zs{t_}" for t_ in range(8)])
                                dump("XB", XB[:, :, :], [f"XB{f_}_{n_}" for f_ in range(6) for n_ in range(2)])
                                dump("XC", XC[:, :, :], [f"XC{f_}_{n_}" for f_ in range(6) for n_ in range(2)])
                                for fb in range(4):
                                    k.ts('dve', DGD[:, fb, :], cstf[:, 0, :],
                                         cv[:, CV_DCOL + g * 4 + fb:CV_DCOL + g * 4 + fb + 1], None, ALU.mult, None,
                                         reads=["cstf", "cv"], writes=["DGD"])
                                if hf == 1:
                                    k.copy('act', prevB[:, 0, :], prevS[:, g, :], reads=[f"prevS{g}"],
                                           writes=["prevB0"])
                                RP = (4, 5, 6, 7)
                                pe_ = nps()
                                da_all = da.rearrange("p a b -> p (a b)")
                                k.mm(ps[pe_][:, 0:64], triu_f, da_all, True, True, reads=["cstf", KA],
                                     writes=[f"ps{pe_}"])
                                k.mm(ps[pe_][:, 64:128], gtl_f, da_all, True, True, reads=["cstf", KA],
                                     writes=[f"ps{pe_}"])
                                k.mm(ps[pe_][:, 128:192], ones_f, da_all, True, True, reads=["cstf", KA],
                                     writes=[f"ps{pe_}"])
                                k.act(E3[:, :, :, :].rearrange("p t a b -> p (t a b)"), ps[pe_][:, 0:192], AF.Exp,
                                      reads=[f"ps{pe_}"], writes=["E3"])
                                for c4 in range(2):
                                    pc = nps()
                                    for cc in range(4):
                                        c = c4 * 4 + cc
                                        cols = slice(c * 128, (c + 1) * 128)
                                        k.mm(ps[pc][:, cc * 128:(cc + 1) * 128], XC[:, 4, cols], XC[:, 5, cols], True, True,
                                             reads=[f"XC4_{c // 4}", f"XC5_{c // 4}"], writes=[f"ps{pc}"])
                                    k.tt('dve', cbm[:, c4 * 4:(c4 + 1) * 4, :],
                                         psf(pc).rearrange("p (a b) -> p a b", a=4), bcast(triu_f, 1, 4), ALU.mult,
                                         reads=[f"ps{pc}", "cstf"], writes=[f"cbm{c4}"])

                                def st_R(c):
                                    for hh2 in range(2):
                                        k.tt('pool', Rh[:, hh2, :, :], bcast(dah[:, c, hh2 * 4:(hh2 + 1) * 4], 2, 128),
                                             bcast(c16[:, 0, :], 1, 4), ALU.mult, reads=[KH, "c16"], writes=[f"Rh{hh2}"])

                                def st_seg(c):
                                    b2 = c % 3
                                    for hh2 in range(2):
                                        pq = nps(RP)
                                        k.mm(psf(pq), c16[:, 1, :], Rh[:, hh2, :, :].rearrange("p a b -> p (a b)"),
                                             True, True, reads=["c16", f"Rh{hh2}"], writes=[f"ps{pq}"])
                                        k.act(E[:, b2, hh2 * 4:(hh2 + 1) * 4, :].rearrange("p a b -> p (a b)"), psf(pq),
                                              AF.Exp, reads=[f"ps{pq}"], writes=[f"E{b2}_{hh2}", f"MT{b2}"])

                                def st_B1(c):
                                    b2 = c % 2
                                    nh = c // 4
                                    cols = slice(c * 128, (c + 1) * 128)
                                    px = nps(RP)
                                    pxv = psb(px)
                                    for fb in range(4):
                                        k.tr(pxv[:, fb * 128:(fb + 1) * 128], XC[:, fb, cols], ident_b,
                                             reads=[f"XC{fb}_{nh}", "cstb"], writes=[f"ps{px}"], signal=(fb == 3))
                                    pbk = nps(RP)
                                    k.tr(psb(pbk)[:, 0:128], XC[:, 4, cols], ident_b, reads=[f"XC4_{nh}", "cstb"],
                                         writes=[f"ps{pbk}"], signal=True)
                                    e3 = c % 3
                                    k.tt('dve', MT[:, e3, :, :], E[:, e3, :, :], bcast(cbm[:, c, :], 1, 8), ALU.mult,
                                         reads=[f"E{e3}_0", f"E{e3}_1", f"cbm{c // 4}"],
                                         writes=[f"MT{e3}", f"E{e3}_0", f"E{e3}_1"])
                                    x3 = pxv[:, 0:512].rearrange("p (a b) -> p a b", a=8)
                                    k.tt('dve', xdt[:, b2, :].rearrange("p (a b) -> p a b", a=8), x3,
                                         bcast(dts[:, c, :], 2, 64), ALU.mult, reads=[f"ps{px}", KD],
                                         writes=[f"xdt{b2}"])
                                    k.tt('dve', xdd[:, b2, :].rearrange("p (a b) -> p a b", a=8),
                                         xdt[:, b2, :].rearrange("p (a b) -> p a b", a=8),
                                         bcast(E3[:, 1, c, :], 2, 64), ALU.mult, reads=[f"xdt{b2}", "E3"],
                                         writes=[f"xdd{b2}"])
                                    k.copy('act', Bc[:, b2, :], psb(pbk)[:, 0:128], reads=[f"ps{pbk}"],
                                           writes=[f"Bc{b2}"])

                                def st_B2(c):
                                    b2 = c % 2
                                    nh = c // 4
                                    cols = slice(c * 128, (c + 1) * 128)
                                    py = b2
                                    for fb in range(4):
                                        k.mm(ps[py][:, fb * 128:(fb + 1) * 128], XC[:, fb, cols], DGD[:, fb, :], True, False,
                                             reads=[f"XC{fb}_{nh}", "DGD"], writes=[f"ps{py}"])
                                        for hh in (2 * fb, 2 * fb + 1):
                                            k.mm(ps[py][:, hh * 64:(hh + 1) * 64], MT[:, c % 3, hh, :],
                                                 xdt[:, b2, hh * 64:(hh + 1) * 64], False, hh == 2 * fb + 1,
                                                 reads=[f"MT{c % 3}", f"xdt{b2}"], writes=[f"ps{py}"])
                                    pst = 2 + b2
                                    k.mm(psf(pst), Bc[:, b2, :], xdd[:, b2, :], True, True,
                                         reads=[f"Bc{b2}", f"xdd{b2}"], writes=[f"ps{pst}"])

                                def st_C_pe(c):
                                    b2 = c % 2
                                    nh = c // 4
                                    cols = slice(c * 128, (c + 1) * 128)
                                    first = (hf == 0 and c == 0)
                                    if first:
                                        return None
                                    po = nps(RP)
                                    k.mm(psf(po), XC[:, 5, cols], prevB[:, b2, :], True, True,
                                         reads=[f"XC5_{nh}", f"prevB{b2}"], writes=[f"ps{po}"])
                                    return po

                                def st_C(c, po):
                                    b2 = c % 2
                                    py = b2
                                    pst = 2 + b2
                                    k.tt('dve', prevS[:, g, :].rearrange("p (a b) -> p a b", a=8),
                                         prevS[:, g, :].rearrange("p (a b) -> p a b", a=8),
                                         bcast(E3[:, 2, c, :], 2, 64), ALU.mult, reads=[f"prevS{g}", "E3"],
                                         writes=[f"prevS{g}"])
                                    k.tt('dve', prevS[:, g, :], prevS[:, g, :], psf(pst), ALU.add,
                                         reads=[f"prevS{g}", f"ps{pst}"], writes=[f"prevS{g}"])
                                    k.copy('act', prevB[:, 1 - b2, :], prevS[:, g, :], reads=[f"prevS{g}"],
                                           writes=[f"prevB{1 - b2}"])
                                    if po is not None:
                                        k.tt('dve', yb[:, 0, :].rearrange("p (a b) -> p a b", a=8),
                                             psf(po).rearrange("p (a b) -> p a b", a=8),
                                             bcast(E3[:, 0, c, :], 2, 64), ALU.mult, reads=[f"ps{po}", "E3"],
                                             writes=["yb0"])
                                        k.tt('dve', yb[:, 0, :], yb[:, 0, :], psf(py), ALU.add,
                                             reads=["yb0", f"ps{py}"], writes=["yb0"])
                                    else:
                                        k.copy('dve', yb[:, 0, :], psf(py), reads=[f"ps{py}"], writes=["yb0"])
                                    gc = gcol[0]
                                    gcol[0] += 1
                                    k.tt('dve', yb[:, 0, :], yb[:, 0, :], Zs[:, c, :], ALU.mult,
                                         reads=["yb0", f"Zs{c}"], writes=["yb0"])
                                    k.act(junk[:, 0:512], yb[:, 0, :], AF.Square, reads=["yb0", "gss"],
                                          writes=["junk", f"gss{gc}"], accum_out=gss[:, gc:gc + 1])
                                    k.act(grs[:, gc:gc + 1], gss[:, gc:gc + 1], AF.Ln, reads=[f"gss{gc}"],
                                          writes=[f"grs{gc}"], scale=1.0 / 512, bias=EPS)
                                    k.act(grs[:, gc:gc + 1], grs[:, gc:gc + 1], AF.Exp, reads=[f"grs{gc}"],
                                          writes=[f"grs{gc}"], scale=-0.5)
                                    k.act(ynb[:, b2, :], yb[:, 0, :], AF.Identity, reads=["yb0", f"grs{gc}"],
                                          writes=[f"ynb{b2}"], scale=grs[:, gc:gc + 1])

                                def st_D(c):
                                    b2 = c % 2
                                    pt = nps(RP)
                                    ptv = psb(pt)
                                    for fb in range(4):
                                        k.tr(ptv[:, fb * 128:(fb + 1) * 128], ynb[:, b2, fb * 128:(fb + 1) * 128],
                                             ident_b, reads=[f"ynb{b2}", "cstb"], writes=[f"ps{pt}"],
                                             signal=(fb == 3))
                                    for fb in range(4):
                                        k.act(YT[:, fb, c * 128:(c + 1) * 128], ptv[:, fb * 128:(fb + 1) * 128],
                                              AF.Identity, reads=[f"ps{pt}", "cv"], writes=[f"YT{c // 4}"],
                                              scale=cv[:, CV_GNORM + g * 4 + fb:CV_GNORM + g * 4 + fb + 1])

                                xgen = None
                                if g < 3:
                                    pl = issue_proj_loads(g + 1)
                                    xgen = xb_proj(g + 1, (pl[1], pl[2]), pool=(4, 5, 6, 7))
                                k.mark(f"L1h{hf}_g{g}_chunks")
                                if g < 3:
                                    dt_path(g + 1, pool=(4, 5, 6, 7))
                                for c0 in range(3):
                                    st_R(c0)
                                    st_seg(c0)
                                st_B1(0)
                                st_B2(0)
                                for i in range(8):
                                    if i + 3 < 8:
                                        st_R(i + 3)
                                    if i + 1 < 8:
                                        st_B1(i + 1)
                                    po = st_C_pe(i)
                                    if i + 1 < 8:
                                        st_B2(i + 1)
                                    st_C(i, po)
                                    if g < 3:
                                        z_tile(g + 1, i, pl[0], pool=(4, 5, 6, 7), silu=False)
                                    if xgen is not None:
                                        for _ in range(2):
                                            if next(xgen, "done") == "done":
                                                xgen = None
                                                break
                                    if i >= 1:
                                        st_D(i - 1)
                                    if i + 3 < 8:
                                        st_seg(i + 3)
                                if xgen is not None:
                                    for _ in xgen:
                                        pass
                                st_D(7)
                                if g == 1:
                                    dump("YT", YT[:, :, :], ["YT0", "YT1"])
                                    dump("E3g", E3[:, :, :], ["E3"])
                                    dump("cbmg", cbm[:, :, :], ["cbm0", "cbm1"])
                                k.mark(f"L1h{hf}_g{g}_outproj")
                                rv = ring_view(so, (4, 1024))
                                for tt in range(8):
                                    for ch in range(2):
                                        pi = nps()
                                        for fb in range(4):
                                            k.mm(psf(pi), YT[:, fb, tt * 128:(tt + 1) * 128],
                                                 rv[:, fb, ch * 512:(ch + 1) * 512], fb == 0, fb == 3,
                                                 reads=[f"ring{so}", f"YT{tt // 4}"], writes=[f"ps{pi}"])
                                        hacc(hf * 8 + tt, ch, pi)
                                if g < 3:
                                    pso = issue_out_load(g + 1)
                            if hf == 1:
                                sq_tiles(list(range(8, 16)))
                            k.barrier()

        def ffn(i):
            k.mark(f"ffn{i}_norm")
            with ExitStack() as st:
                fT = sb("fT", [128, 8, 2048], BF16, st)
                hid = sb("hid", [128, 2, 4, 512], BF16, st)
                rl = sb("rl", [128, 2, 512], F32, st)
                norm_T(lambda j: h[:, j, :], lambda j: f"h{j}", 16, CV_GFFN + 8 * i, fT, lambda j: f"fT{j}",
                       cols=[presq.pop(j) for j in range(16)] if all(j in presq for j in range(16)) else None)
                FK = [f"fT{j}" for j in range(16)]
                hb = 0
                ri = 0
                k.mark(f"ffn{i}_body")
                units = {}

                def f_load(he):
                    a = load_unit(wcols(w_ffn1[i], he * 512))
                    b_ = load_unit(w_ffn2[i][he * 512:(he + 1) * 512, :].rearrange("(c p) n -> p c n", p=128),
                                   view=(4, 1024))
                    units[he] = (a, b_)

                def f_up(he, tq, hbuf):
                    s1 = units[he][0]
                    for fbl in range(4):
                        pi = nps()
                        for c in range(8):
                            k.mm(psf(pi), ring[s1][:, c, fbl * 128:(fbl + 1) * 128],
                                 fT[:, c, tq * 512:(tq + 1) * 512],
                                 c == 0, c == 7, reads=[f"ring{s1}"] + FK[tq * 4:tq * 4 + 4],
                                 writes=[f"ps{pi}"])
                        rb = state['rl'] % 2
                        state['rl'] += 1
                        k.ts('dve', rl[:, rb, :], psf(pi), 0.0, None, ALU.max, None, reads=[f"ps{pi}"],
                             writes=[f"rl{rb}"])
                        k.act(hid[:, hbuf, fbl, :], rl[:, rb, :], AF.Square, reads=[f"rl{rb}"],
                              writes=[f"hid{hbuf}_{fbl}"])

                def f_down(he, tq, hbuf):
                    s2 = units[he][1]
                    rv2 = ring_view(s2, (4, 1024))
                    for t4 in range(4):
                        for ch in range(2):
                            pi = nps()
                            for fbl in range(4):
                                k.mm(psf(pi), hid[:, hbuf, fbl, t4 * 128:(t4 + 1) * 128],
                                     rv2[:, fbl, ch * 512:(ch + 1) * 512], fbl == 0, fbl == 3,
                                     reads=[f"ring{s2}", f"hid{hbuf}_{fbl}"], writes=[f"ps{pi}"])
                            hacc(tq * 4 + t4, ch, pi)

                its = [(he, tq) for he in range(8) for tq in range(4)]
                f_load(0)
                f_up(0, 0, 0)
                for n_, (he, tq) in enumerate(its):
                    if n_ + 1 < len(its):
                        he2, tq2 = its[n_ + 1]
                        if he2 not in units:
                            f_load(he2)
                        f_up(he2, tq2, (n_ + 1) % 2)
                    f_down(he, tq, n_ % 2)
                    if he == 7:
                        sq_tiles(list(range(tq * 4, tq * 4 + 4)))
                k.barrier()

        for i in range(n_layers):
            if 'mem' in phases:
                phase_mem(i)
            if i == 0:
                if 'mix' in phases:
                    layer0_mixer()
            else:
                if not skip_l1_mixer:
                    layer1_mixer()
            if 'ffn' in phases:
                ffn(i)

        k.mark("final")
        gfin = sb("gfin", [128, 1024], F32)
        k.dma('sp', out=gfin[:, :], in_=gfin_d[:, :], sem="gf", writes=["gfin"])
        if all(tt in presq for tt in range(NT)):
            fcols = [presq.pop(tt) for tt in range(NT)]
        else:
            c0 = state['col']
            state['col'] += NT
            for tt in range(NT):
                k.act(junk[:, :], h[:, tt, :], AF.Square, reads=[f"h{tt}", "ssq"], writes=["junk", f"ssq{c0 + tt}"],
                      accum_out=ssq[:, c0 + tt:c0 + tt + 1])
            rkeys = [f"rstd{c0 + j}" for j in range(NT)]
            k.act(rstd[:, c0:c0 + NT], ssq[:, c0:c0 + NT], AF.Ln, reads=[f"ssq{c0 + j}" for j in range(NT)],
                  writes=rkeys, scale=1.0 / D, bias=EPS)
            k.act(rstd[:, c0:c0 + NT], rstd[:, c0:c0 + NT], AF.Exp, reads=rkeys, writes=rkeys, scale=-0.5)
            fcols = [c0 + tt for tt in range(NT)]
        for tt in range(NT):
            col = fcols[tt]
            k.stt('dve', h[:, tt, :], h[:, tt, :], rstd[:, col:col + 1], gfin[:, :],
                  ALU.mult, ALU.mult, reads=[f"h{tt}", f"rstd{col}", "gfin"], writes=[f"h{tt}"])
            k.dma('sp', out=out[tt * 128:(tt + 1) * 128, :], in_=h[:, tt, :], sem="out", reads=[f"h{tt}"])
        nc.sync.wait_ge(k.sem["out"], k.cnt["out"])
        if "dbg" in k.sem:
            nc.sync.wait_ge(k.sem["dbg"], k.cnt["dbg"])
        global MARKS
        MARKS = k.marks + [("end", k.npe)]
        print("build: waits", k.nwait, "counts", {e: k.cnt[e] for e in ['pe', 'act', 'dve', 'pool']})
    return nc


def host_consts(inp):
    f = np.float32
    cv = np.zeros((128, NCV), f)

    def pp(vec):
        return np.ascontiguousarray(np.asarray(vec, f).reshape(-1, 128).T)

    for i in range(2):
        cv[:, CV_GMIX + 8 * i:CV_GMIX + 8 * i + 8] = pp(inp["norm_mix"][i])
        cv[:, CV_GFFN + 8 * i:CV_GFFN + 8 * i + 8] = pp(inp["norm_ffn"][i])
        cv[:, CV_GMEM + 8 * i:CV_GMEM + 8 * i + 8] = pp(inp["mem_norm"][i])
    cv[:, CV_LNG:CV_LNG + 16] = pp(inp["a_ln_g"][0])
    cv[:, CV_GNORM:CV_GNORM + 16] = pp(inp["b_gnorm"][0])
    cv[:, CV_CONVB:CV_CONVB + 24] = pp(inp["b_conv_b"][0])
    cw = np.asarray(inp["b_conv_w"][0], f)
    cv[:, CV_CONVW:CV_CONVW + 96] = cw.reshape(4, 24, 128).transpose(2, 1, 0).reshape(128, 96)
    cv[:, CV_DCOL:CV_DCOL + 16] = pp(np.repeat(np.asarray(inp["b_d"][0], f), 64))
    bc = np.zeros((128, NBC), f)
    gfin = np.ascontiguousarray(np.broadcast_to(np.asarray(inp["final_norm"], f)[None, :], (128, 1024)))
    bc[:, BC_D:BC_D + 32] = np.asarray(inp["b_d"][0], f)[None, :]
    bc[:, BC_DTB:BC_DTB + 32] = np.asarray(inp["b_dt_bias"][0], f)[None, :]
    bc[:, BC_ALOG:BC_ALOG + 32] = np.asarray(inp["b_a_log"][0], f)[None, :]
    lnb = np.ascontiguousarray(np.broadcast_to(np.asarray(inp["a_ln_b"][0], f)[None, :], (128, 2048)))
    wsT = np.ascontiguousarray(np.asarray(inp["a_ws"][0], f).transpose(2, 0, 1))
    bsr = np.ascontiguousarray(np.asarray(inp["a_bs"][0], f).reshape(1, 1024))
    cst = np.zeros((128, 4, 128), f)
    cst[:, 0, :] = np.eye(128, dtype=f)
    cst[:, 1, :] = np.triu(np.ones((128, 128), f))
    cst[:, 2, :] = np.tril(np.ones((128, 128), f), -1)
    cst[:, 3, :] = 1.0
    return dict(cv=cv, bc=bc, wsT=wsT, bsr=bsr, cst=cst, lnb=lnb, gfin=gfin)


_CACHE = {}


def kernel(**inputs):
    inp = {k_: np.asarray(v) for k_, v in inputs.items()}
    hc = host_consts(inp)
    if "nc" not in _CACHE:
        _CACHE["nc"] = build_program()
    nc = _CACHE["nc"]
    shared = dict(
        w_kv=np.ascontiguousarray(inp["w_kv"], dtype=np.float32),
        w_out=np.ascontiguousarray(inp["w_out"], dtype=np.float32),
        w_ffn1=np.ascontiguousarray(inp["w_ffn1"], dtype=np.float32),
        w_ffn2=np.ascontiguousarray(inp["w_ffn2"], dtype=np.float32),
        a_in=np.ascontiguousarray(inp["a_in"][0], dtype=np.float32),
        b_in=np.ascontiguousarray(inp["b_in"][0], dtype=np.float32),
        **hc,
    )
    in_maps = []
    for b in range(8):
        m = dict(shared)
        m["x"] = np.ascontiguousarray(inp["x"][b], dtype=np.float32)
        m["mem"] = np.ascontiguousarray(inp["mem"][b], dtype=np.float32)
        in_maps.append(m)
    res = run_bass_kernel_spmd(nc, in_maps, core_ids=list(range(8)))
    return np.stack([np.asarray(r["out"], dtype=np.float32) for r in res.results], axis=0)
```

```python
import numpy as np
from contextlib import ExitStack
import concourse.bass as bass
import concourse.mybir as mybir
from concourse.bass_utils import run_bass_kernel_spmd

F32 = mybir.dt.float32
BF16 = mybir.dt.bfloat16
FP16 = mybir.dt.float16
AF = mybir.ActivationFunctionType
ALU = mybir.AluOpType
AX = mybir.AxisListType

T = 2048
D = 1024
NT = 16
EPS = 1e-6
NSLOT = 4
STOP = 0
MARKS = []
DEBUG = False

CV_GMIX, CV_GFFN, CV_GMEM, CV_LNG, CV_GNORM, CV_CONVB, CV_CONVW, CV_DCOL, NCV = 0, 16, 32, 48, 64, 80, 104, 200, 216
BC_D, BC_DTB, BC_ALOG, NBC = 0, 32, 64, 96


class KB:
    def __init__(self, nc, es):
        self.nc = nc
        self.es = es
        self.eng = {'pe': nc.tensor, 'act': nc.scalar, 'dve': nc.vector, 'pool': nc.gpsimd, 'sp': nc.sync}
        self.sem = {}
        self.cnt = {}
        for e in ['pe', 'act', 'dve', 'pool']:
            self.sem[e] = es.enter_context(nc.semaphore('s_' + e))
            self.cnt[e] = 0
        self.obs = {e: {} for e in self.eng}
        self.lw = {}
        self.rd = {}
        self.nwait = 0
        self.npe = 0
        self.marks = []

    def dma_sem(self, name):
        if name not in self.sem:
            self.sem[name] = self.es.enter_context(self.nc.semaphore('d_' + name))
            self.cnt[name] = 0
        return name

    def _deps(self, e, reads, writes):
        deps = {}
        for k in reads:
            m = self.lw.get(k)
            if m is not None:
                deps[m[0]] = max(deps.get(m[0], 0), m[1])
        for k in writes:
            m = self.lw.get(k)
            if m is not None:
                deps[m[0]] = max(deps.get(m[0], 0), m[1])
            for s, v in self.rd.get(k, {}).items():
                deps[s] = max(deps.get(s, 0), v)
        for s, v in deps.items():
            if s == e and e == 'pe':
                continue
            if self.obs[e].get(s, 0) >= v:
                continue
            assert v <= self.cnt[s], f"dep on pending signal {s}:{v} > {self.cnt[s]}"
            self.eng[e].wait_ge(self.sem[s], v)
            self.obs[e][s] = v
            self.nwait += 1

    def op(self, e, fn, reads=(), writes=(), signal=True):
        self._deps(e, reads, writes)
        ins = fn()
        mark = (e, self.cnt[e] + 1)
        if signal:
            ins.then_inc(self.sem[e], 1)
            self.cnt[e] += 1
        for k in writes:
            self.lw[k] = mark
            self.rd[k] = {}
        for k in reads:
            d = self.rd.setdefault(k, {})
            d[e] = max(d.get(e, 0), mark[1])
        return ins

    def dma(self, q, out, in_, sem, reads=(), writes=()):
        self.dma_sem(sem)
        self._deps(q, reads, writes)
        self.eng[q].dma_start(out=out, in_=in_).then_inc(self.sem[sem], 16)
        self.cnt[sem] += 16
        mark = (sem, self.cnt[sem])
        for k in writes:
            self.lw[k] = mark
            self.rd[k] = {}
        for k in reads:
            d = self.rd.setdefault(k, {})
            d[sem] = max(d.get(sem, 0), mark[1])

    def barrier(self):
        for e in self.eng:
            if e == 'pool':
                continue
            for s in list(self.sem.keys()):
                if s == e:
                    continue
                v = self.cnt[s]
                if v > 0 and self.obs[e].get(s, 0) < v:
                    self.eng[e].wait_ge(self.sem[s], v)
                    self.obs[e][s] = v

    def mark(self, name):
        self.marks.append((name, self.npe))

    def mm(self, out, lhsT, rhs, start, stop, reads, writes):
        self.npe += 2 if lhsT.dtype == F32 else 1
        return self.op('pe', lambda: self.nc.tensor.matmul(out, lhsT, rhs, start=start, stop=stop),
                       reads=reads, writes=writes, signal=stop)

    def tr(self, out, in_, ident, reads, writes, signal):
        self.npe += 1
        return self.op('pe', lambda: self.nc.tensor.transpose(out, in_, ident),
                       reads=reads, writes=writes, signal=signal)

    def act(self, out, in_, func, reads, writes, bias=None, scale=None, accum_out=None):
        kw = {}
        if bias is not None:
            kw['bias'] = bias
        if scale is not None:
            kw['scale'] = scale
        if accum_out is not None:
            kw['accum_out'] = accum_out
        return self.op('act', lambda: self.nc.scalar.activation(out, in_, func, **kw), reads=reads, writes=writes)

    def tt(self, e, out, in0, in1, op, reads, writes):
        return self.op(e, lambda: self.eng[e].tensor_tensor(out, in0, in1, op), reads=reads, writes=writes)

    def ts(self, e, out, in0, s1, s2, op0, op1, reads, writes):
        if op1 is None:
            s2, op1 = 0.0, ALU.add
        return self.op(e, lambda: self.eng[e].tensor_scalar(out, in0, s1, s2, op0, op1), reads=reads, writes=writes)

    def stt(self, e, out, in0, scalar, in1, op0, op1, reads, writes):
        return self.op(e, lambda: self.eng[e].scalar_tensor_tensor(out, in0, scalar, in1, op0, op1),
                       reads=reads, writes=writes)

    def copy(self, e, out, in_, reads, writes):
        if e == 'act':
            return self.op('act', lambda: self.nc.scalar.copy(out, in_), reads=reads, writes=writes)
        return self.op(e, lambda: self.eng[e].tensor_copy(out, in_), reads=reads, writes=writes)


def bcast(ap, axis, n):
    a = ap.unsqueeze(axis)
    shp = list(a.shape)
    shp[axis] = n
    return a.broadcast_to(shp)


def build_program(dbg=None, skip_l1_mixer=False, n_layers=2, phases=('mem', 'mix', 'ffn')):
    nc = bass.Bass("TRN2", target_bir_lowering=False)
    dr = {}

    def din(name, shape):
        dr[name] = nc.dram_tensor(name, list(shape), F32, kind="ExternalInput").ap()
        return dr[name]

    x = din("x", [T, D])
    mem = din("mem", [256, D])
    w_kv = din("w_kv", [2, 1024, 2048])
    w_out = din("w_out", [2, 3072, 1024])
    w_ffn1 = din("w_ffn1", [2, 1024, 4096])
    w_ffn2 = din("w_ffn2", [2, 4096, 1024])
    a_in = din("a_in", [1024, 5120])
    b_in = din("b_in", [1024, 6176])
    cv_d = din("cv", [128, NCV])
    bc_d = din("bc", [128, NBC])
    wsT_d = din("wsT", [128, 8, 128])
    bsr_d = din("bsr", [1, 1024])
    cst_d = din("cst", [128, 4, 128])
    lnb_d = din("lnb", [128, 2048])
    gfin_d = din("gfin", [128, 1024])
    out = nc.dram_tensor("out", [T, D], F32, kind="ExternalOutput").ap()
    dbg_out = None
    if dbg is not None:
        dbg_out = nc.dram_tensor("dbg", list(dbg[1]), F32, kind="ExternalOutput").ap()

    with ExitStack() as es:
        k = KB(nc, es)

        uid = [0]
        dumped = set()

        def dump(tag, ap, reads):
            if not DEBUG or tag in dumped:
                return
            dumped.add(tag)
            dt_ = nc.dram_tensor("dbg_" + tag, list(ap.shape), ap.dtype, kind="ExternalOutput").ap()
            k.dma('sp', out=dt_, in_=ap, sem="dbg", reads=reads)

        def sb(name, shape, dt, stack=es):
            uid[0] += 1
            return stack.enter_context(nc.sbuf_tensor(f"sb_{name}_{uid[0]}", list(shape), dt))

        h = sb("h", [128, NT, D], F32)
        ring = [sb(f"ring{i}", [128, 8, 512], BF16) for i in range(NSLOT)]
        cv = sb("cv", [128, NCV], F32)
        bc = sb("bc", [128, NBC], F32)
        cstf = sb("cstf", [128, 4, 128], F32)
        cstb = sb("cstb", [128, 4, 128], BF16)
        kT = sb("kT", [128, 8, 256], BF16)
        vmem = sb("vmem", [128, 2, 1024], BF16)
        wdt = sb("wdt", [128, 8, 32], BF16)
        ssq = sb("ssq", [128, 128], F32)
        rstd = sb("rstd", [128, 128], F32)
        ybf = [sb(f"ybf{i}", [128, 1024], BF16) for i in range(2)]
        junk = sb("junk", [128, 1024], BF16)
        ps = [es.enter_context(nc.psum_tensor(f"ps{i}", [128, 512], F32)) for i in range(8)]
        ident_b = cstb[:, 0, :]
        triu_b = cstb[:, 1, :]
        ones_b = cstb[:, 3, :]
        gtl_b = cstb[:, 2, :]
        triu_f = cstf[:, 1, :]
        gtl_f = cstf[:, 2, :]
        ones_f = cstf[:, 3, :]

        state = {'ps': 0, 'slot': 0, 'col': 0, 'yb': 0, 'sv': 0, 'rl': 0}

        def nps(pool=(0, 1, 2, 3, 4, 5, 6, 7)):
            i = pool[state['ps'] % len(pool)]
            state['ps'] += 1
            return i

        def psf(i):
            return ps[i][:, :]

        def psb(i):
            return ps[i][:, :].bitcast(BF16)

        def load_unit(src_ap, view=None):
            s = state['slot'] % NSLOT
            state['slot'] += 1
            dst = ring[s][:, :, :] if view is None else ring_view(s, view)
            k.dma('pool', out=dst, in_=src_ap, sem=f"r{s}", writes=[f"ring{s}"])
            return s

        def ring_view(s, view):
            a, b = view
            return ring[s][:, :, :].rearrange("p c n -> p (c n)").rearrange("p (a b) -> p a b", a=a)

        def wcols(w2d, c0, ncols=512):
            return w2d[:, c0:c0 + ncols].rearrange("(c p) n -> p c n", p=128)

        k.dma('sp', out=cv[:, :], in_=cv_d[:, :], sem="c0", writes=["cv"])
        k.dma('sp', out=bc[:, :], in_=bc_d[:, :], sem="c0", writes=["bc"])
        k.dma('sp', out=cstf[:, :, :], in_=cst_d[:, :, :], sem="c0", writes=["cstf"])
        k.dma('pool', out=cstb[:, :, :], in_=cst_d[:, :, :], sem="c1", writes=["cstb"])
        for key in ["cv", "bc", "cstf"]:
            k.lw[key] = ("c0", k.cnt["c0"])
        k.dma('pool', out=wdt[:, :, :], in_=b_in[:, 5120:5152].rearrange("(c p) n -> p c n", p=128),
              sem="wd", writes=["wdt"])
        x_v = x.rearrange("(t p) d -> p t d", p=128)

        def load_x(gs=(0, 1, 2, 3)):
            for g in gs:
                k.dma('sp', out=h[:, 4 * g:4 * g + 4, :], in_=x_v[:, 4 * g:4 * g + 4, :], sem=f"x{g}",
                      writes=[f"h{t}" for t in range(4 * g, 4 * g + 4)])

        k.op('dve', lambda: nc.vector.memset(ssq[:, :], 0.0), writes=["ssq"])

        presq = {}

        def sq_tiles(tiles):
            n = len(tiles)
            c0 = state['col']
            state['col'] += n
            for j, tt in enumerate(tiles):
                k.act(junk[:, :], h[:, tt, :], AF.Square, reads=[f"h{tt}", "ssq"], writes=["junk", f"ssq{c0 + j}"],
                      accum_out=ssq[:, c0 + j:c0 + j + 1])
            rkeys = [f"rstd{c0 + j}" for j in range(n)]
            k.act(rstd[:, c0:c0 + n], ssq[:, c0:c0 + n], AF.Ln, reads=[f"ssq{c0 + j}" for j in range(n)],
                  writes=rkeys, scale=1.0 / D, bias=EPS)
            k.act(rstd[:, c0:c0 + n], rstd[:, c0:c0 + n], AF.Exp, reads=rkeys, writes=rkeys, scale=-0.5)
            for j, tt in enumerate(tiles):
                presq[tt] = c0 + j

        def norm_T(src_fn, src_key_fn, ntiles, gcol, dst, dst_key_fn, dcols=D, cols=None):
            if cols is None:
                c0 = state['col']
                state['col'] += ntiles
                for j in range(ntiles):
                    k.act(junk[:, :], src_fn(j), AF.Square, reads=[src_key_fn(j), "ssq"],
                          writes=["junk", f"ssq{c0 + j}"], accum_out=ssq[:, c0 + j:c0 + j + 1])
                rkeys = [f"rstd{c0 + j}" for j in range(ntiles)]
                k.act(rstd[:, c0:c0 + ntiles], ssq[:, c0:c0 + ntiles], AF.Ln,
                      reads=[f"ssq{c0 + j}" for j in range(ntiles)], writes=rkeys, scale=1.0 / dcols, bias=EPS)
                k.act(rstd[:, c0:c0 + ntiles], rstd[:, c0:c0 + ntiles], AF.Exp, reads=rkeys, writes=rkeys, scale=-0.5)
                cols = [c0 + j for j in range(ntiles)]

            def stage1(j):
                col = cols[j]
                yb = state['yb'] % 2
                state['yb'] += 1
                k.ts('dve', ybf[yb][:, :], src_fn(j), rstd[:, col:col + 1], None, ALU.mult, None,
                     reads=[src_key_fn(j), f"rstd{col}"], writes=[f"ybf{yb}"])
                pi = nps()
                pv = psb(pi).rearrange("p (c n) -> p c n", c=8)
                for c in range(8):
                    k.tr(pv[:, c, :], ybf[yb][:, c * 128:(c + 1) * 128], ident_b,
                         reads=[f"ybf{yb}", "cstb"], writes=[f"ps{pi}"], signal=(c == 7))
                return pi, pv

            def stage2(j, pi, pv):
                k.tt('dve', dst[:, :, j * 128:(j + 1) * 128], pv,
                     bcast(cv[:, gcol:gcol + 8], 2, 128), ALU.mult,
                     reads=[f"ps{pi}", "cv"], writes=[dst_key_fn(j)])

            prev = stage1(0)
            for j in range(ntiles):
                nxt = stage1(j + 1) if j + 1 < ntiles else None
                stage2(j, *prev)
                prev = nxt

        def hacc(tt, ch, pi):
            k.tt('dve', h[:, tt, ch * 512:(ch + 1) * 512], h[:, tt, ch * 512:(ch + 1) * 512], psf(pi), ALU.add,
                 reads=[f"ps{pi}", f"h{tt}"], writes=[f"h{tt}"])

        def phase_mem(i):
            k.mark(f"mem{i}")
            with ExitStack() as st:
                memt = sb("memt", [128, 2, 1024], F32, st)
                mT = sb("mT", [128, 8, 256], BF16, st)
                k.dma('sp', out=memt[:, :, :], in_=mem.rearrange("(t p) d -> p t d", p=128), sem="mm",
                      writes=["memt0", "memt1"])
                if i == 0:
                    load_x((0, 1))
                norm_T(lambda j: memt[:, j, :], lambda j: f"memt{j}", 2, CV_GMEM + 8 * i, mT,
                       lambda j: f"mT{j}")
                for ku in range(2):
                    s = load_unit(wcols(w_kv[i], ku * 512))
                    for fb in range(4):
                        pi = nps()
                        for c in range(8):
                            k.mm(ps[pi][:, 0:256], ring[s][:, c, fb * 128:(fb + 1) * 128], mT[:, c, :],
                                 c == 0, c == 7, reads=[f"ring{s}", "mT0", "mT1"], writes=[f"ps{pi}"])
                        k.copy('act', kT[:, ku * 4 + fb, :], ps[pi][:, 0:256], reads=[f"ps{pi}"],
                               writes=[f"kT{ku * 4 + fb}"])
                for vu in range(2):
                    s = load_unit(wcols(w_kv[i], 1024 + vu * 512))
                    for mt in range(2):
                        pi = nps()
                        for c in range(8):
                            k.mm(psf(pi), mT[:, c, mt * 128:(mt + 1) * 128], ring[s][:, c, :],
                                 c == 0, c == 7, reads=[f"ring{s}", f"mT{mt}"], writes=[f"ps{pi}"])
                        k.copy('act', vmem[:, mt, vu * 512:(vu + 1) * 512], psf(pi), reads=[f"ps{pi}"],
                               writes=[f"vmem{mt}_{vu}"])
                k.barrier()

        VM_KEYS = [f"vmem{mt}_{vu}" for mt in range(2) for vu in range(2)]

        def attention(i, hf, xT, qcol_w, qcol0, st):
            qT = sb("qT", [128, 2, 2, 1024], BF16, st)
            moT = sb("moT", [128, 8, 1024], BF16, st)
            PT = sb("PT", [128, 2, 2, 512], BF16, st)
            rden = sb("rden", [128, 2, 512], F32, st)
            XK = [f"xT{j}" for j in range(8)]
            pcount = 0
            for qu in range(2):
                s = load_unit(wcols(qcol_w, qcol0 + qu * 512))
                for hl in range(2):
                    hh = 2 * qu + hl
                    qb = hh % 2
                    for fb in range(2):
                        for nh in range(2):
                            pi = nps()
                            for c in range(8):
                                k.mm(psf(pi), ring[s][:, c, hl * 256 + fb * 128: hl * 256 + (fb + 1) * 128],
                                     xT[:, c, nh * 512:(nh + 1) * 512], c == 0, c == 7,
                                     reads=[f"ring{s}"] + XK[nh * 4:nh * 4 + 4], writes=[f"ps{pi}"])
                            k.copy('act', qT[:, qb, fb, nh * 512:(nh + 1) * 512], psf(pi), reads=[f"ps{pi}"],
                                   writes=[f"qT{qb}_{fb}_{nh}"])
                    for nh in range(2):
                        pb = pcount % 2
                        pcount += 1
                        for mb in range(2):
                            pi = nps()
                            for fb in range(2):
                                k.mm(psf(pi), kT[:, hh * 2 + fb, mb * 128:(mb + 1) * 128],
                                     qT[:, qb, fb, nh * 512:(nh + 1) * 512], fb == 0, fb == 1,
                                     reads=[f"kT{hh * 2 + fb}", f"qT{qb}_{fb}_{nh}"], writes=[f"ps{pi}"])
                            k.act(PT[:, pb, mb, :], psf(pi), AF.Exp, reads=[f"ps{pi}"], writes=[f"PT{pb}_{mb}"],
                                  scale=1.0 / 16.0)
                        pi = nps()
                        for mb in range(2):
                            k.mm(psf(pi), ones_b, PT[:, pb, mb, :], mb == 0, mb == 1,
                                 reads=["cstb", f"PT{pb}_{mb}"], writes=[f"ps{pi}"])
                        k.op('dve', lambda: nc.vector.reciprocal(rden[:, pb, :], psf(pi)), reads=[f"ps{pi}"],
                             writes=[f"rden{pb}"])
                        for fb in range(2):
                            pi = nps()
                            for mb in range(2):
                                k.mm(psf(pi), vmem[:, mb, hh * 256 + fb * 128: hh * 256 + (fb + 1) * 128],
                                     PT[:, pb, mb, :], mb == 0, mb == 1,
                                     reads=VM_KEYS + [f"PT{pb}_{mb}"], writes=[f"ps{pi}"])
                            k.tt('dve', moT[:, hh * 2 + fb, nh * 512:(nh + 1) * 512], psf(pi), rden[:, pb, :],
                                 ALU.mult, reads=[f"ps{pi}", f"rden{pb}"], writes=[f"moT{hh * 2 + fb}_{nh}"])
            su = [load_unit(w_out[i][2048:3072, ch * 512:(ch + 1) * 512].rearrange("(c p) n -> p c n", p=128))
                  for ch in range(2)]
            for tt in range(8):
                for ch in range(2):
                    pi = nps()
                    for c in range(8):
                        k.mm(psf(pi), moT[:, c, tt * 128:(tt + 1) * 128], ring[su[ch]][:, c, :], c == 0, c == 7,
                             reads=[f"ring{su[ch]}", f"moT{c}_{tt // 4}"], writes=[f"ps{pi}"])
                    hacc(hf * 8 + tt, ch, pi)

        def layer0_mixer():
            with ExitStack() as L:
                WT = sb("WT", [128, 8, 128], BF16, L)
                Cst = sb("Cst", [128, 16, 128], F32, L)
                with ExitStack() as st:
                    wsf = sb("wsf", [128, 8, 128], F32, st)
                    lnb = sb("lnb", [128, 2048], BF16, st)
                    bsr = sb("bsr", [1, 1024], BF16, st)
                    bsrf = sb("bsrf", [1, 1024], F32, st)
                    k.dma('sp', out=wsf[:, :, :], in_=wsT_d[:, :, :], sem="m0", writes=["wsf"])
                    k.dma('sp', out=bsrf[:, :], in_=bsr_d[:, :], sem="m1", writes=["bsrf"])
                    k.op('dve', lambda: nc.vector.tensor_copy(bsr[:, :], bsrf[:, :]), reads=["bsrf"], writes=["bsr"])
                    lnbf = sb("lnbf", [128, 2048], F32, st)
                    k.dma('sp', out=lnbf[:, :], in_=lnb_d[:, :], sem="m2", writes=["lnbf"])
                    load_x((2, 3))
                    k.op('dve', lambda: nc.vector.tensor_copy(lnb[:, :], lnbf[:, :]),
                         reads=["lnbf"], writes=["lnb"])
                    k.tt('dve', WT[:, :, :], wsf[:, :, :], bcast(triu_f, 1, 8), ALU.mult,
                         reads=["wsf", "cstf"], writes=["WT"])
                    for db in range(16):
                        g = db // 2
                        pi = nps()
                        k.mm(ps[pi][:, 0:128], lnb[:, db * 128:(db + 1) * 128], WT[:, g, :], True, False,
                             reads=["lnb", "WT"], writes=[f"ps{pi}"])
                        k.mm(ps[pi][:, 0:128], ones_b[0:1, :], bsr[0:1, g * 128:(g + 1) * 128], False, True,
                             reads=["cstb", "bsr"], writes=[f"ps{pi}"])
                        k.copy('act', Cst[:, db, :], ps[pi][:, 0:128], reads=[f"ps{pi}"], writes=["Cst"])
                    k.barrier()
                if STOP == 1:
                    return
                for hf in range(2):
                    with ExitStack() as H:
                        k.mark(f"L0h{hf}_norm")
                        xT = sb("xT", [128, 8, 1024], BF16, H)
                        norm_T(lambda j: h[:, hf * 8 + j, :], lambda j: f"h{hf * 8 + j}", 8, CV_GMIX, xT,
                               lambda j: f"xT{j}", cols=[presq.pop(hf * 8 + j) for j in range(8)]
                               if all((hf * 8 + j) in presq for j in range(8)) else None)
                        if hf == 0:
                            sq_tiles(list(range(8, 16)))
                        else:
                            sq_tiles(list(range(0, 8)))
                        XK = [f"xT{j}" for j in range(8)]
                        if STOP == 2:
                            return
                        with ExitStack() as st:
                            k.mark(f"L0h{hf}_attn")
                            attention(0, hf, xT, a_in, 4096, st)
                            k.barrier()
                        if STOP == 3:
                            return
                        with ExitStack() as st:
                            V = sb("V", [128, 8, 2048], BF16, st)
                            U = sb("U", [128, 2, 4, 1024], BF16, st)
                            svt = sb("svt", [128, 2, 512], F32, st)
                            stats = sb("stats", [128, 8, 4, 6], F32, st)
                            mv = sb("mv", [128, 8, 2], F32, st)
                            lr = sb("lr", [128, 8, 2], F32, st)
                            k.mark(f"L0h{hf}_vproj")
                            for vu in range(4):
                                s = load_unit(wcols(a_in, 2048 + vu * 512))
                                for tt in range(8):
                                    pi = nps()
                                    for c in range(8):
                                        k.mm(psf(pi), xT[:, c, tt * 128:(tt + 1) * 128], ring[s][:, c, :],
                                             c == 0, c == 7, reads=[f"ring{s}", f"xT{tt}"], writes=[f"ps{pi}"])
                                    k.act(V[:, tt, vu * 512:(vu + 1) * 512], psf(pi), AF.Gelu, reads=[f"ps{pi}"],
                                          writes=[f"V{tt}"])
                            if STOP == 4:
                                return
                            for tt in range(8):
                                for q4 in range(4):
                                    k.op('dve', lambda: nc.vector.bn_stats(stats[:, tt, q4, :],
                                                                           V[:, tt, q4 * 512:(q4 + 1) * 512]),
                                         reads=[f"V{tt}"], writes=[f"stats{tt}"])
                                k.op('dve', lambda: nc.vector.bn_aggr(mv[:, tt, :], stats[:, tt, :, :]),
                                     reads=[f"stats{tt}"], writes=[f"mv{tt}"])
                            k.act(lr[:, :, 0], mv[:, :, 1], AF.Ln, reads=[f"mv{tt}" for tt in range(8)],
                                  writes=[f"lr{tt}" for tt in range(8)], bias=EPS)
                            k.act(lr[:, :, 0], lr[:, :, 0], AF.Exp, reads=[f"lr{tt}" for tt in range(8)],
                                  writes=[f"lr{tt}" for tt in range(8)], scale=-0.5)
                            for tt in range(8):
                                k.stt('dve', lr[:, tt, 1:2], mv[:, tt, 0:1], -1.0, lr[:, tt, 0:1], ALU.mult,
                                      ALU.mult, reads=[f"mv{tt}", f"lr{tt}"], writes=[f"lr{tt}"])
                                k.ts('dve', V[:, tt, :], V[:, tt, :], lr[:, tt, 0:1], lr[:, tt, 1:2], ALU.mult,
                                     ALU.add, reads=[f"V{tt}", f"lr{tt}"], writes=[f"V{tt}"])
                            if STOP == 5:
                                return
                            gl = {}

                            def g_loads(gp):
                                a = load_unit(wcols(a_in, gp * 512))
                                b_ = load_unit(w_out[0][gp * 512:(gp + 1) * 512, :].rearrange(
                                    "(c p) n -> p c n", p=128), view=(4, 1024))
                                gl[gp] = (a, b_)

                            def g_uproj(gp):
                                k.mark(f"L0h{hf}_gp{gp}")
                                ub = gp % 2
                                s_ = gl[gp][0]
                                for fb in range(4):
                                    for nh in range(2):
                                        pi = nps()
                                        for c in range(8):
                                            k.mm(psf(pi), ring[s_][:, c, fb * 128:(fb + 1) * 128],
                                                 xT[:, c, nh * 512:(nh + 1) * 512], c == 0, c == 7,
                                                 reads=[f"ring{s_}"] + XK[nh * 4:nh * 4 + 4], writes=[f"ps{pi}"])
                                        k.act(U[:, ub, fb, nh * 512:(nh + 1) * 512], psf(pi), AF.Gelu,
                                              reads=[f"ps{pi}"], writes=[f"U{ub}_{fb}_{nh}"])

                            def g_gate(gp):
                                ub = gp % 2
                                for fb in range(4):
                                    db = gp * 4 + fb
                                    g = db // 2
                                    for cq in range(2):
                                        pi = nps()
                                        for c4 in range(4):
                                            ch = cq * 4 + c4
                                            k.mm(ps[pi][:, c4 * 128:(c4 + 1) * 128],
                                                 V[:, ch, db * 128:(db + 1) * 128], WT[:, g, :], True, True,
                                                 reads=[f"V{ch}", "WT"], writes=[f"ps{pi}"])
                                        sb_i = state['sv'] % 2
                                        state['sv'] += 1
                                        k.act(svt[:, sb_i, :], psf(pi), AF.Identity, reads=[f"ps{pi}", "cv"],
                                              writes=[f"svt{sb_i}"], scale=cv[:, CV_LNG + db:CV_LNG + db + 1])
                                        k.tt('dve', svt[:, sb_i, :].rearrange("p (a b) -> p a b", a=4),
                                             svt[:, sb_i, :].rearrange("p (a b) -> p a b", a=4),
                                             bcast(Cst[:, db, :], 1, 4), ALU.add,
                                             reads=[f"svt{sb_i}", "Cst"], writes=[f"svt{sb_i}"])
                                        k.tt('dve', U[:, ub, fb, cq * 512:(cq + 1) * 512],
                                             U[:, ub, fb, cq * 512:(cq + 1) * 512], svt[:, sb_i, :], ALU.mult,
                                             reads=[f"svt{sb_i}", f"U{ub}_{fb}_{cq}"], writes=[f"U{ub}_{fb}_{cq}"])

                            def g_out(gp):
                                ub = gp % 2
                                so_ = gl[gp][1]
                                rv = ring_view(so_, (4, 1024))
                                for tt in range(8):
                                    for ch in range(2):
                                        pi = nps()
                                        for fb in range(4):
                                            k.mm(psf(pi), U[:, ub, fb, tt * 128:(tt + 1) * 128],
                                                 rv[:, fb, ch * 512:(ch + 1) * 512], fb == 0, fb == 3,
                                                 reads=[f"ring{so_}", f"U{ub}_{fb}_{tt // 4}"], writes=[f"ps{pi}"])
                                        hacc(hf * 8 + tt, ch, pi)

                            g_loads(0)
                            g_uproj(0)
                            g_gate(0)
                            for gp in range(1, 4):
                                g_loads(gp)
                                g_uproj(gp)
                                g_out(gp - 1)
                                g_gate(gp)
                            g_out(3)
                            if hf == 1:
                                sq_tiles(list(range(8, 16)))
                            k.barrier()


        def layer1_mixer():
            with ExitStack() as L:
                HAL = sb("HAL", [128, 24, 3], BF16, L)
                prevS = sb("prevS", [128, 4, 512], F32, L)
                A_b = sb("A_b", [128, 32], F32, L)
                gss = sb("gss", [128, 64], F32, L)
                grs = sb("grs", [128, 64], F32, L)
                k.op('dve', lambda: nc.vector.memset(HAL[:, :, :], 0.0), writes=["HAL"])
                k.op('dve', lambda: nc.vector.memset(prevS[:, :, :], 0.0), writes=[f"prevS{g}" for g in range(4)])
                k.op('dve', lambda: nc.vector.memset(gss[:, :], 0.0), writes=["gss"])
                k.act(A_b[:, :], bc[:, BC_ALOG:BC_ALOG + 32], AF.Exp, reads=["bc"], writes=["A_b"])
                k.ts('dve', A_b[:, :], A_b[:, :], -1.0, None, ALU.mult, None, reads=["A_b"], writes=["A_b"])
                gcol = [0]
                for hf in range(2):
                    with ExitStack() as H:
                        k.mark(f"L1h{hf}_norm")
                        xT = sb("xT", [128, 8, 1024], BF16, H)
                        norm_T(lambda j: h[:, hf * 8 + j, :], lambda j: f"h{hf * 8 + j}", 8, CV_GMIX + 8, xT,
                               lambda j: f"xT{j}", cols=[presq.pop(hf * 8 + j) for j in range(8)]
                               if all((hf * 8 + j) in presq for j in range(8)) else None)
                        if hf == 1:
                            sq_tiles(list(range(0, 8)))
                        XK = [f"xT{j}" for j in range(8)]
                        with ExitStack() as st:
                            k.mark(f"L1h{hf}_attn")
                            attention(1, hf, xT, b_in, 5152, st)
                            k.barrier()
                        with ExitStack() as st:
                            Zs = sb("Zs", [128, 8, 512], BF16, st)
                            XB = sb("XB", [128, 6, 1027], BF16, st)
                            XC = sb("XC", [128, 6, 1024], BF16, st)
                            DG = sb("DG", [128, 2, 4, 128], BF16, st)
                            dtsA = sb("dts", [128, 2, 8, 8], F32, st)
                            daA = sb("da", [128, 2, 8, 8], F32, st)
                            Rh = sb("Rh", [128, 2, 4, 128], FP16, st)
                            c16 = sb("c16", [128, 2, 128], FP16, st)
                            dahA = sb("dah", [128, 2, 8, 8], FP16, st)
                            E3 = sb("E3", [128, 3, 8, 8], F32, st)
                            cbm = sb("cbm", [128, 8, 128], BF16, st)
                            E = sb("E", [128, 3, 8, 128], BF16, st)
                            MT = E
                            xdt = sb("xdt", [128, 2, 512], BF16, st)
                            DGD = sb("DGD", [128, 4, 128], BF16, st)
                            xdd = sb("xdd", [128, 2, 512], BF16, st)
                            Bc = sb("Bc", [128, 2, 128], BF16, st)
                            yb = sb("yb", [128, 1, 512], F32, st)
                            ynb = sb("ynb", [128, 2, 512], BF16, st)
                            prevB = sb("prevB", [128, 2, 512], BF16, st)
                            YT = sb("YT", [128, 4, 1024], BF16, st)
                            k.copy('dve', c16[:, :, :], cstf[:, 1:3, :], reads=["cstf"], writes=["c16"])
                            for g in range(4):
                                k.mark(f"L1h{hf}_g{g}_proj")
                                def issue_proj_loads(gg):
                                    a = load_unit(wcols(b_in, gg * 512))
                                    b_ = load_unit(wcols(b_in, 2048 + gg * 512))
                                    c_ = state['slot'] % NSLOT
                                    state['slot'] += 1
                                    k.dma('pool', out=ring[c_][:, :, 0:128],
                                          in_=wcols(b_in, 4096 + gg * 128, 128), sem=f"r{c_}", writes=[f"ring{c_}"])
                                    k.dma('pool', out=ring[c_][:, :, 128:256],
                                          in_=wcols(b_in, 4608 + gg * 128, 128), sem=f"r{c_}", writes=[f"ring{c_}"])
                                    return a, b_, c_

                                def issue_out_load(gg):
                                    return load_unit(w_out[1][gg * 512:(gg + 1) * 512, :].rearrange(
                                        "(c p) n -> p c n", p=128), view=(4, 1024))

                                if g == 0:
                                    pl = issue_proj_loads(0)
                                    pso = issue_out_load(0)
                                sz, sx, s3 = pl
                                so = pso
                                gpar = g % 2
                                dts = dtsA[:, gpar, :, :]
                                da = daA[:, gpar, :, :]
                                dah = dahA[:, gpar, :, :]
                                KD, KA, KH = f"dts{gpar}", f"da{gpar}", f"dah{gpar}"

                                def dt_path(gg, pool=None):
                                    gp_ = gg % 2
                                    pi = nps(pool) if pool else nps()
                                    for tt in range(8):
                                        for c in range(8):
                                            k.mm(ps[pi][:, tt * 8:(tt + 1) * 8], xT[:, c, tt * 128:(tt + 1) * 128],
                                                 wdt[:, c, gg * 8:(gg + 1) * 8], c == 0, c == 7,
                                                 reads=["wdt", f"xT{tt}"], writes=[f"ps{pi}"])
                                    k.tt('dve', dtsA[:, gp_, :, :], ps[pi][:, 0:64].rearrange("p (a b) -> p a b", a=8),
                                         bcast(bc[:, BC_DTB + gg * 8:BC_DTB + gg * 8 + 8], 1, 8), ALU.add,
                                         reads=[f"ps{pi}", "bc"], writes=[f"dts{gp_}"])
                                    k.act(dtsA[:, gp_, :, :], dtsA[:, gp_, :, :], AF.Exp, reads=[f"dts{gp_}"],
                                          writes=[f"dts{gp_}"])
                                    k.act(dtsA[:, gp_, :, :], dtsA[:, gp_, :, :], AF.Ln, reads=[f"dts{gp_}"],
                                          writes=[f"dts{gp_}"], bias=1.0)
                                    k.tt('dve', daA[:, gp_, :, :], dtsA[:, gp_, :, :],
                                         bcast(A_b[:, gg * 8:(gg + 1) * 8], 1, 8), ALU.mult,
                                         reads=[f"dts{gp_}", "A_b"], writes=[f"da{gp_}"])
                                    k.copy('pool', dahA[:, gp_, :, :], daA[:, gp_, :, :], reads=[f"da{gp_}"],
                                           writes=[f"dah{gp_}"])

                                def z_tile(gg, tt, slot, pool=None, silu=True):
                                    pi = nps(pool) if pool else nps()
                                    for c in range(8):
                                        k.mm(psf(pi), xT[:, c, tt * 128:(tt + 1) * 128], ring[slot][:, c, :],
                                             c == 0, c == 7, reads=[f"ring{slot}", f"xT{tt}"], writes=[f"ps{pi}"])
                                    k.act(Zs[:, tt, :], psf(pi), AF.Silu if silu else AF.Identity, reads=[f"ps{pi}"],
                                          writes=[f"Zs{tt}"])

                                if g == 0:
                                    dt_path(0)
                                    for tt in range(8):
                                        z_tile(0, tt, sz)
                                else:
                                    for tt in range(8):
                                        k.act(Zs[:, tt, :], Zs[:, tt, :], AF.Silu, reads=[f"Zs{tt}"], writes=[f"Zs{tt}"])
                                blks = [g * 4 + fb for fb in range(4)] + [16 + g, 20 + g]

                                def xb_proj(gg, slots, pool=None):
                                    sx_, s3_ = slots
                                    bl = [gg * 4 + fb for fb in range(4)] + [16 + gg, 20 + gg]
                                    for fb in range(6):
                                        if fb < 4:
                                            wsl = lambda c: ring[sx_][:, c, fb * 128:(fb + 1) * 128]
                                            wk = f"ring{sx_}"
                                        else:
                                            wsl = lambda c: ring[s3_][:, c, (fb - 4) * 128:(fb - 3) * 128]
                                            wk = f"ring{s3_}"
                                        k.copy('dve', XB[:, fb, 0:3], HAL[:, bl[fb], :], reads=["HAL"],
                                               writes=[f"XB{fb}_0"])
                                        for nh in range(2):
                                            pi = nps(pool) if pool else nps()
                                            for c in range(8):
                                                k.mm(psf(pi), wsl(c), xT[:, c, nh * 512:(nh + 1) * 512], c == 0, c == 7,
                                                     reads=[wk] + XK[nh * 4:nh * 4 + 4], writes=[f"ps{pi}"])
                                            k.copy('dve', XB[:, fb, 3 + nh * 512:3 + (nh + 1) * 512], psf(pi),
                                                   reads=[f"ps{pi}"], writes=[f"XB{fb}_{nh}"])
                                            if nh == 1:
                                                k.copy('dve', HAL[:, bl[fb], :], XB[:, fb, 1024:1027],
                                                       reads=[f"XB{fb}_1"], writes=["HAL"])
                                            yield

                                if g == 0:
                                    for _ in xb_proj(0, (sx, s3)):
                                        pass
                                for fb in range(6):
                                    blk = blks[fb]
                                    dgi = fb % 2
                                    for kk in range(4):
                                        k.ts('dve', DG[:, dgi, kk, :], cstf[:, 0, :],
                                             cv[:, CV_CONVW + blk * 4 + kk:CV_CONVW + blk * 4 + kk + 1], None,
                                             ALU.mult, None, reads=["cstf", "cv"], writes=[f"DG{dgi}"])
                                    for nh in range(2):
                                        pi = nps()
                                        for kk in range(4):
                                            k.mm(psf(pi), DG[:, dgi, kk, :],
                                                 XB[:, fb, kk + nh * 512:kk + nh * 512 + 512], kk == 0, kk == 3,
                                                 reads=[f"DG{dgi}", f"XB{fb}_0", f"XB{fb}_1"] if nh else
                                                 [f"DG{dgi}", f"XB{fb}_0"] + [f"XB{fb}_0"],
                                                 writes=[f"ps{pi}"])
                                        k.act(XC[:, fb, nh * 512:(nh + 1) * 512], psf(pi), AF.Silu,
                                              reads=[f"ps{pi}", "cv"], writes=[f"XC{fb}_{nh}"],
                                              bias=cv[:, CV_CONVB + blk:CV_CONVB + blk + 1])
                                dump("Zs", Zs[:, :, :], [f"Zs{t_}" for t_ in range(8)])
                                dump("XB", XB[:, :, :], [f"XB{f_}_{n_}" for f_ in range(6) for n_ in range(2)])
                                dump("XC", XC[:, :, :], [f"XC{f_}_{n_}" for f_ in range(6) for n_ in range(2)])
                                for fb in range(4):
                                    k.ts('dve', DGD[:, fb, :], cstf[:, 0, :],
                                         cv[:, CV_DCOL + g * 4 + fb:CV_DCOL + g * 4 + fb + 1], None, ALU.mult, None,
                                         reads=["cstf", "cv"], writes=["DGD"])
                                if hf == 1:
                                    k.copy('act', prevB[:, 0, :], prevS[:, g, :], reads=[f"prevS{g}"],
                                           writes=["prevB0"])
                                RP = (4, 5, 6, 7)
                                pe_ = nps()
                                da_all = da.rearrange("p a b -> p (a b)")
                                k.mm(ps[pe_][:, 0:64], triu_f, da_all, True, True, reads=["cstf", KA],
                                     writes=[f"ps{pe_}"])
                                k.mm(ps[pe_][:, 64:128], gtl_f, da_all, True, True, reads=["cstf", KA],
                                     writes=[f"ps{pe_}"])
                                k.mm(ps[pe_][:, 128:192], ones_f, da_all, True, True, reads=["cstf", KA],
                                     writes=[f"ps{pe_}"])
                                k.act(E3[:, :, :, :].rearrange("p t a b -> p (t a b)"), ps[pe_][:, 0:192], AF.Exp,
                                      reads=[f"ps{pe_}"], writes=["E3"])
                                for c4 in range(2):
                                    pc = nps()
                                    for cc in range(4):
                                        c = c4 * 4 + cc
                                        cols = slice(c * 128, (c + 1) * 128)
                                        k.mm(ps[pc][:, cc * 128:(cc + 1) * 128], XC[:, 4, cols], XC[:, 5, cols], True, True,
                                             reads=[f"XC4_{c // 4}", f"XC5_{c // 4}"], writes=[f"ps{pc}"])
                                    k.tt('dve', cbm[:, c4 * 4:(c4 + 1) * 4, :],
                                         psf(pc).rearrange("p (a b) -> p a b", a=4), bcast(triu_f, 1, 4), ALU.mult,
                                         reads=[f"ps{pc}", "cstf"], writes=[f"cbm{c4}"])

                                def st_R(c):
                                    for hh2 in range(2):
                                        k.tt('pool', Rh[:, hh2, :, :], bcast(dah[:, c, hh2 * 4:(hh2 + 1) * 4], 2, 128),
                                             bcast(c16[:, 0, :], 1, 4), ALU.mult, reads=[KH, "c16"], writes=[f"Rh{hh2}"])

                                def st_seg(c):
                                    b2 = c % 3
                                    for hh2 in range(2):
                                        pq = nps(RP)
                                        k.mm(psf(pq), c16[:, 1, :], Rh[:, hh2, :, :].rearrange("p a b -> p (a b)"),
                                             True, True, reads=["c16", f"Rh{hh2}"], writes=[f"ps{pq}"])
                                        k.act(E[:, b2, hh2 * 4:(hh2 + 1) * 4, :].rearrange("p a b -> p (a b)"), psf(pq),
                                              AF.Exp, reads=[f"ps{pq}"], writes=[f"E{b2}_{hh2}", f"MT{b2}"])

                                def st_B1(c):
                                    b2 = c % 2
                                    nh = c // 4
                                    cols = slice(c * 128, (c + 1) * 128)
                                    px = nps(RP)
                                    pxv = psb(px)
                                    for fb in range(4):
                                        k.tr(pxv[:, fb * 128:(fb + 1) * 128], XC[:, fb, cols], ident_b,
                                             reads=[f"XC{fb}_{nh}", "cstb"], writes=[f"ps{px}"], signal=(fb == 3))
                                    pbk = nps(RP)
                                    k.tr(psb(pbk)[:, 0:128], XC[:, 4, cols], ident_b, reads=[f"XC4_{nh}", "cstb"],
                                         writes=[f"ps{pbk}"], signal=True)
                                    e3 = c % 3
                                    k.tt('dve', MT[:, e3, :, :], E[:, e3, :, :], bcast(cbm[:, c, :], 1, 8), ALU.mult,
                                         reads=[f"E{e3}_0", f"E{e3}_1", f"cbm{c // 4}"],
                                         writes=[f"MT{e3}", f"E{e3}_0", f"E{e3}_1"])
                                    x3 = pxv[:, 0:512].rearrange("p (a b) -> p a b", a=8)
                                    k.tt('dve', xdt[:, b2, :].rearrange("p (a b) -> p a b", a=8), x3,
                                         bcast(dts[:, c, :], 2, 64), ALU.mult, reads=[f"ps{px}", KD],
                                         writes=[f"xdt{b2}"])
                                    k.tt('dve', xdd[:, b2, :].rearrange("p (a b) -> p a b", a=8),
                                         xdt[:, b2, :].rearrange("p (a b) -> p a b", a=8),
                                         bcast(E3[:, 1, c, :], 2, 64), ALU.mult, reads=[f"xdt{b2}", "E3"],
                                         writes=[f"xdd{b2}"])
                                    k.copy('act', Bc[:, b2, :], psb(pbk)[:, 0:128], reads=[f"ps{pbk}"],
                                           writes=[f"Bc{b2}"])

                                def st_B2(c):
                                    b2 = c % 2
                                    nh = c // 4
                                    cols = slice(c * 128, (c + 1) * 128)
                                    py = b2
                                    for fb in range(4):
                                        k.mm(ps[py][:, fb * 128:(fb + 1) * 128], XC[:, fb, cols], DGD[:, fb, :], True, False,
                                             reads=[f"XC{fb}_{nh}", "DGD"], writes=[f"ps{py}"])
                                        for hh in (2 * fb, 2 * fb + 1):
                                            k.mm(ps[py][:, hh * 64:(hh + 1) * 64], MT[:, c % 3, hh, :],
                                                 xdt[:, b2, hh * 64:(hh + 1) * 64], False, hh == 2 * fb + 1,
                                                 reads=[f"MT{c % 3}", f"xdt{b2}"], writes=[f"ps{py}"])
                                    pst = 2 + b2
                                    k.mm(psf(pst), Bc[:, b2, :], xdd[:, b2, :], True, True,
                                         reads=[f"Bc{b2}", f"xdd{b2}"], writes=[f"ps{pst}"])

                                def st_C_pe(c):
                                    b2 = c % 2
                                    nh = c // 4
                                    cols = slice(c * 128, (c + 1) * 128)
                                    first = (hf == 0 and c == 0)
                                    if first:
                                        return None
                                    po = nps(RP)
                                    k.mm(psf(po), XC[:, 5, cols], prevB[:, b2, :], True, True,
                                         reads=[f"XC5_{nh}", f"prevB{b2}"], writes=[f"ps{po}"])
                                    return po

                                def st_C(c, po):
                                    b2 = c % 2
                                    py = b2
                                    pst = 2 + b2
                                    k.tt('dve', prevS[:, g, :].rearrange("p (a b) -> p a b", a=8),
                                         prevS[:, g, :].rearrange("p (a b) -> p a b", a=8),
                                         bcast(E3[:, 2, c, :], 2, 64), ALU.mult, reads=[f"prevS{g}", "E3"],
                                         writes=[f"prevS{g}"])
                                    k.tt('dve', prevS[:, g, :], prevS[:, g, :], psf(pst), ALU.add,
                                         reads=[f"prevS{g}", f"ps{pst}"], writes=[f"prevS{g}"])
                                    k.copy('act', prevB[:, 1 - b2, :], prevS[:, g, :], reads=[f"prevS{g}"],
                                           writes=[f"prevB{1 - b2}"])
                                    if po is not None:
                                        k.tt('dve', yb[:, 0, :].rearrange("p (a b) -> p a b", a=8),
                                             psf(po).rearrange("p (a b) -> p a b", a=8),
                                             bcast(E3[:, 0, c, :], 2, 64), ALU.mult, reads=[f"ps{po}", "E3"],
                                             writes=["yb0"])
                                        k.tt('dve', yb[:, 0, :], yb[:, 0, :], psf(py), ALU.add,
                                             reads=["yb0", f"ps{py}"], writes=["yb0"])
                                    else:
                                        k.copy('dve', yb[:, 0, :], psf(py), reads=[f"ps{py}"], writes=["yb0"])
                                    gc = gcol[0]
                                    gcol[0] += 1
                                    k.tt('dve', yb[:, 0, :], yb[:, 0, :], Zs[:, c, :], ALU.mult,
                                         reads=["yb0", f"Zs{c}"], writes=["yb0"])
                                    k.act(junk[:, 0:512], yb[:, 0, :], AF.Square, reads=["yb0", "gss"],
                                          writes=["junk", f"gss{gc}"], accum_out=gss[:, gc:gc + 1])
                                    k.act(grs[:, gc:gc + 1], gss[:, gc:gc + 1], AF.Ln, reads=[f"gss{gc}"],
                                          writes=[f"grs{gc}"], scale=1.0 / 512, bias=EPS)
                                    k.act(grs[:, gc:gc + 1], grs[:, gc:gc + 1], AF.Exp, reads=[f"grs{gc}"],
                                          writes=[f"grs{gc}"], scale=-0.5)
                                    k.act(ynb[:, b2, :], yb[:, 0, :], AF.Identity, reads=["yb0", f"grs{gc}"],
                                          writes=[f"ynb{b2}"], scale=grs[:, gc:gc + 1])

                                def st_D(c):
                                    b2 = c % 2
                                    pt = nps(RP)
                                    ptv = psb(pt)
                                    for fb in range(4):
                                        k.tr(ptv[:, fb * 128:(fb + 1) * 128], ynb[:, b2, fb * 128:(fb + 1) * 128],
                                             ident_b, reads=[f"ynb{b2}", "cstb"], writes=[f"ps{pt}"],
                                             signal=(fb == 3))
                                    for fb in range(4):
                                        k.act(YT[:, fb, c * 128:(c + 1) * 128], ptv[:, fb * 128:(fb + 1) * 128],
                                              AF.Identity, reads=[f"ps{pt}", "cv"], writes=[f"YT{c // 4}"],
                                              scale=cv[:, CV_GNORM + g * 4 + fb:CV_GNORM + g * 4 + fb + 1])

                                xgen = None
                                if g < 3:
                                    pl = issue_proj_loads(g + 1)
                                    xgen = xb_proj(g + 1, (pl[1], pl[2]), pool=(4, 5, 6, 7))
                                k.mark(f"L1h{hf}_g{g}_chunks")
                                if g < 3:
                                    dt_path(g + 1, pool=(4, 5, 6, 7))
                                for c0 in range(3):
                                    st_R(c0)
                                    st_seg(c0)
                                st_B1(0)
                                st_B2(0)
                                for i in range(8):
                                    if i + 3 < 8:
                                        st_R(i + 3)
                                    if i + 1 < 8:
                                        st_B1(i + 1)
                                    po = st_C_pe(i)
                                    if i + 1 < 8:
                                        st_B2(i + 1)
                                    st_C(i, po)
                                    if g < 3:
                                        z_tile(g + 1, i, pl[0], pool=(4, 5, 6, 7), silu=False)
                                    if xgen is not None:
                                        for _ in range(2):
                                            if next(xgen, "done") == "done":
                                                xgen = None
                                                break
                                    if i >= 1:
                                        st_D(i - 1)
                                    if i + 3 < 8:
                                        st_seg(i + 3)
                                if xgen is not None:
                                    for _ in xgen:
                                        pass
                                st_D(7)
                                if g == 1:
                                    dump("YT", YT[:, :, :], ["YT0", "YT1"])
                                    dump("E3g", E3[:, :, :], ["E3"])
                                    dump("cbmg", cbm[:, :, :], ["cbm0", "cbm1"])
                                k.mark(f"L1h{hf}_g{g}_outproj")
                                rv = ring_view(so, (4, 1024))
                                for tt in range(8):
                                    for ch in range(2):
                                        pi = nps()
                                        for fb in range(4):
                                            k.mm(psf(pi), YT[:, fb, tt * 128:(tt + 1) * 128],
                                                 rv[:, fb, ch * 512:(ch + 1) * 512], fb == 0, fb == 3,
                                                 reads=[f"ring{so}", f"YT{tt // 4}"], writes=[f"ps{pi}"])
                                        hacc(hf * 8 + tt, ch, pi)
                                if g < 3:
                                    pso = issue_out_load(g + 1)
                            if hf == 1:
                                sq_tiles(list(range(8, 16)))
                            k.barrier()

        def ffn(i):
            k.mark(f"ffn{i}_norm")
            with ExitStack() as st:
                fT = sb("fT", [128, 8, 2048], BF16, st)
                last = (i == n_layers - 1)
                if last:
                    gfl = sb("gfl", [128, 1024], F32, st)
                    k.dma('sp', out=gfl[:, :], in_=gfin_d[:, :], sem="gf", writes=["gfl"])
                hid = sb("hid", [128, 2, 4, 512], BF16, st)
                rl = sb("rl", [128, 2, 512], F32, st)
                norm_T(lambda j: h[:, j, :], lambda j: f"h{j}", 16, CV_GFFN + 8 * i, fT, lambda j: f"fT{j}",
                       cols=[presq.pop(j) for j in range(16)] if all(j in presq for j in range(16)) else None)
                FK = [f"fT{j}" for j in range(16)]
                hb = 0
                ri = 0
                k.mark(f"ffn{i}_body")
                units = {}

                def f_load(he):
                    a = load_unit(wcols(w_ffn1[i], he * 512))
                    b_ = load_unit(w_ffn2[i][he * 512:(he + 1) * 512, :].rearrange("(c p) n -> p c n", p=128),
                                   view=(4, 1024))
                    units[he] = (a, b_)

                def f_up(he, tq, hbuf):
                    s1 = units[he][0]
                    for fbl in range(4):
                        pi = nps()
                        for c in range(8):
                            k.mm(psf(pi), ring[s1][:, c, fbl * 128:(fbl + 1) * 128],
                                 fT[:, c, tq * 512:(tq + 1) * 512],
                                 c == 0, c == 7, reads=[f"ring{s1}"] + FK[tq * 4:tq * 4 + 4],
                                 writes=[f"ps{pi}"])
                        rb = state['rl'] % 2
                        state['rl'] += 1
                        k.ts('dve', rl[:, rb, :], psf(pi), 0.0, None, ALU.max, None, reads=[f"ps{pi}"],
                             writes=[f"rl{rb}"])
                        k.act(hid[:, hbuf, fbl, :], rl[:, rb, :], AF.Square, reads=[f"rl{rb}"],
                              writes=[f"hid{hbuf}_{fbl}"])

                def f_down(he, tq, hbuf):
                    s2 = units[he][1]
                    rv2 = ring_view(s2, (4, 1024))
                    for t4 in range(4):
                        for ch in range(2):
                            pi = nps()
                            for fbl in range(4):
                                k.mm(psf(pi), hid[:, hbuf, fbl, t4 * 128:(t4 + 1) * 128],
                                     rv2[:, fbl, ch * 512:(ch + 1) * 512], fbl == 0, fbl == 3,
                                     reads=[f"ring{s2}", f"hid{hbuf}_{fbl}"], writes=[f"ps{pi}"])
                            hacc(tq * 4 + t4, ch, pi)

                its = [(he, tq) for he in range(8) for tq in range(4)]
                f_load(0)
                f_up(0, 0, 0)
                for n_, (he, tq) in enumerate(its):
                    if n_ + 1 < len(its):
                        he2, tq2 = its[n_ + 1]
                        if he2 not in units:
                            f_load(he2)
                        f_up(he2, tq2, (n_ + 1) % 2)
                    f_down(he, tq, n_ % 2)
                    if he == 7:
                        sq_tiles(list(range(tq * 4, tq * 4 + 4)))
                        if last:
                            for tt in range(tq * 4, tq * 4 + 4):
                                col = presq.pop(tt)
                                k.stt('dve', h[:, tt, :], h[:, tt, :], rstd[:, col:col + 1], gfl[:, :],
                                      ALU.mult, ALU.mult, reads=[f"h{tt}", f"rstd{col}", "gfl"], writes=[f"h{tt}"])
                                k.dma('sp', out=out[tt * 128:(tt + 1) * 128, :], in_=h[:, tt, :], sem="out",
                                      reads=[f"h{tt}"])
                            state['final_done'] = True
                k.barrier()

        for i in range(n_layers):
            if 'mem' in phases:
                phase_mem(i)
            if i == 0:
                if 'mix' in phases:
                    layer0_mixer()
            else:
                if not skip_l1_mixer:
                    layer1_mixer()
            if 'ffn' in phases:
                ffn(i)

        if not state.get('final_done'):
            k.mark("final")
            gfin = sb("gfin", [128, 1024], F32)
            k.dma('sp', out=gfin[:, :], in_=gfin_d[:, :], sem="gf", writes=["gfin"])
            if all(tt in presq for tt in range(NT)):
                fcols = [presq.pop(tt) for tt in range(NT)]
            else:
                c0 = state['col']
                state['col'] += NT
                for tt in range(NT):
                    k.act(junk[:, :], h[:, tt, :], AF.Square, reads=[f"h{tt}", "ssq"], writes=["junk", f"ssq{c0 + tt}"],
                          accum_out=ssq[:, c0 + tt:c0 + tt + 1])
                rkeys = [f"rstd{c0 + j}" for j in range(NT)]
                k.act(rstd[:, c0:c0 + NT], ssq[:, c0:c0 + NT], AF.Ln, reads=[f"ssq{c0 + j}" for j in range(NT)],
                      writes=rkeys, scale=1.0 / D, bias=EPS)
                k.act(rstd[:, c0:c0 + NT], rstd[:, c0:c0 + NT], AF.Exp, reads=rkeys, writes=rkeys, scale=-0.5)
                fcols = [c0 + tt for tt in range(NT)]
            for tt in range(NT):
                col = fcols[tt]
                k.stt('dve', h[:, tt, :], h[:, tt, :], rstd[:, col:col + 1], gfin[:, :],
                      ALU.mult, ALU.mult, reads=[f"h{tt}", f"rstd{col}", "gfin"], writes=[f"h{tt}"])
                k.dma('sp', out=out[tt * 128:(tt + 1) * 128, :], in_=h[:, tt, :], sem="out", reads=[f"h{tt}"])
        nc.sync.wait_ge(k.sem["out"], k.cnt["out"])
        if "dbg" in k.sem:
            nc.sync.wait_ge(k.sem["dbg"], k.cnt["dbg"])
        global MARKS
        MARKS = k.marks + [("end", k.npe)]
        print("build: waits", k.nwait, "counts", {e: k.cnt[e] for e in ['pe', 'act', 'dve', 'pool']})
    return nc


def host_consts(inp):
    f = np.float32
    cv = np.zeros((128, NCV), f)

    def pp(vec):
        return np.ascontiguousarray(np.asarray(vec, f).reshape(-1, 128).T)

    for i in range(2):
        cv[:, CV_GMIX + 8 * i:CV_GMIX + 8 * i + 8] = pp(inp["norm_mix"][i])
        cv[:, CV_GFFN + 8 * i:CV_GFFN + 8 * i + 8] = pp(inp["norm_ffn"][i])
        cv[:, CV_GMEM + 8 * i:CV_GMEM + 8 * i + 8] = pp(inp["mem_norm"][i])
    cv[:, CV_LNG:CV_LNG + 16] = pp(inp["a_ln_g"][0])
    cv[:, CV_GNORM:CV_GNORM + 16] = pp(inp["b_gnorm"][0])
    cv[:, CV_CONVB:CV_CONVB + 24] = pp(inp["b_conv_b"][0])
    cw = np.asarray(inp["b_conv_w"][0], f)
    cv[:, CV_CONVW:CV_CONVW + 96] = cw.reshape(4, 24, 128).transpose(2, 1, 0).reshape(128, 96)
    cv[:, CV_DCOL:CV_DCOL + 16] = pp(np.repeat(np.asarray(inp["b_d"][0], f), 64))
    bc = np.zeros((128, NBC), f)
    gfin = np.ascontiguousarray(np.broadcast_to(np.asarray(inp["final_norm"], f)[None, :], (128, 1024)))
    bc[:, BC_D:BC_D + 32] = np.asarray(inp["b_d"][0], f)[None, :]
    bc[:, BC_DTB:BC_DTB + 32] = np.asarray(inp["b_dt_bias"][0], f)[None, :]
    bc[:, BC_ALOG:BC_ALOG + 32] = np.asarray(inp["b_a_log"][0], f)[None, :]
    lnb = np.ascontiguousarray(np.broadcast_to(np.asarray(inp["a_ln_b"][0], f)[None, :], (128, 2048)))
    wsT = np.ascontiguousarray(np.asarray(inp["a_ws"][0], f).transpose(2, 0, 1))
    bsr = np.ascontiguousarray(np.asarray(inp["a_bs"][0], f).reshape(1, 1024))
    cst = np.zeros((128, 4, 128), f)
    cst[:, 0, :] = np.eye(128, dtype=f)
    cst[:, 1, :] = np.triu(np.ones((128, 128), f))
    cst[:, 2, :] = np.tril(np.ones((128, 128), f), -1)
    cst[:, 3, :] = 1.0
    return dict(cv=cv, bc=bc, wsT=wsT, bsr=bsr, cst=cst, lnb=lnb, gfin=gfin)


_CACHE = {}


def kernel(**inputs):
    inp = {k_: np.asarray(v) for k_, v in inputs.items()}
    hc = host_consts(inp)
    if "nc" not in _CACHE:
        _CACHE["nc"] = build_program()
    nc = _CACHE["nc"]
    shared = dict(
        w_kv=np.ascontiguousarray(inp["w_kv"], dtype=np.float32),
        w_out=np.ascontiguousarray(inp["w_out"], dtype=np.float32),
        w_ffn1=np.ascontiguousarray(inp["w_ffn1"], dtype=np.float32),
        w_ffn2=np.ascontiguousarray(inp["w_ffn2"], dtype=np.float32),
        a_in=np.ascontiguousarray(inp["a_in"][0], dtype=np.float32),
        b_in=np.ascontiguousarray(inp["b_in"][0], dtype=np.float32),
        **hc,
    )
    in_maps = []
    for b in range(8):
        m = dict(shared)
        m["x"] = np.ascontiguousarray(inp["x"][b], dtype=np.float32)
        m["mem"] = np.ascontiguousarray(inp["mem"][b], dtype=np.float32)
        in_maps.append(m)
    res = run_bass_kernel_spmd(nc, in_maps, core_ids=list(range(8)))
    return np.stack([np.asarray(r["out"], dtype=np.float32) for r in res.results], axis=0)
```
